# Optimizing a Trainium2 kernel written in Bass

```python
import jax, jax.numpy as jnp
from jax import lax
import numpy as np

D_MODEL = 1024
BATCH = 8
SEQ = 4096
DEPTH = 2
DEC_BATCH = 32
DEC_SEQ = 64
PAST_LEN = 1024

CHUNK = 64
EPS = 1e-6
W_A = D_MODEL
K_A = 3
W_B = D_MODEL
K_B = 31
W_C = D_MODEL
POOL_WINDOWS = (2, 4, 8, 16)
N_POOL_GROUPS = 4
POOL_GROUP = W_C // N_POOL_GROUPS
POOL_PAD = max(POOL_WINDOWS) - 1
N_BRANCH = 3
SPLITS = (W_A, 2 * W_A, 3 * W_A, 3 * W_A + W_B, 3 * W_A + 2 * W_B, 3 * W_A + 2 * W_B + W_C)
N_IN = 3 * W_A + 2 * W_B + W_C + N_BRANCH * D_MODEL
D_FF = -(-8 * D_MODEL // (3 * 256)) * 256

kernel_name = "hybrid_streaming_conv_pool_encoder_step"


def rms_norm(x, g):
    xf = x.astype(jnp.float32)
    y = xf * lax.rsqrt(jnp.mean(xf * xf, axis=-1, keepdims=True) + EPS)
    return (y * g.astype(jnp.float32)).astype(x.dtype)


def layer_norm(x, g, b):
    xf = x.astype(jnp.float32)
    mu = jnp.mean(xf, axis=-1, keepdims=True)
    var = jnp.mean(jnp.square(xf - mu), axis=-1, keepdims=True)
    y = (xf - mu) * lax.rsqrt(var + EPS)
    return (y * g.astype(jnp.float32) + b.astype(jnp.float32)).astype(x.dtype)


def causal_depthwise(full, w):
    c = w.shape[1]
    return lax.conv_general_dilated(full, w[:, None, :], window_strides=(1,), padding='VALID',
                                    dimension_numbers=('NWC', 'WIO', 'NWC'), feature_group_count=c)


def multiscale_pool(full, pos0):
    s = full.shape[1] - POOL_PAD
    f = full.astype(jnp.float32)
    cs = jnp.cumsum(f, axis=1)
    cs = jnp.concatenate([jnp.zeros_like(cs[:, :1]), cs], axis=1)
    end = POOL_PAD + 1
    xcur = f[:, POOL_PAD:]
    pos = pos0 + jnp.arange(s)
    outs = []
    for g, w in enumerate(POOL_WINDOWS):
        sl = slice(g * POOL_GROUP, (g + 1) * POOL_GROUP)
        wsum = cs[:, end:end + s, sl] - cs[:, end - w:end - w + s, sl]
        cnt = jnp.minimum(pos + 1, w).astype(jnp.float32)[None, :, None]
        outs.append(wsum / cnt - xcur[:, :, sl])
    return jnp.stack(outs, axis=2).astype(full.dtype)


def mixer_branches(h, prev_a, prev_b, prev_p, pos0, w_in, w_conv_a, w_out_a, w_conv_b, b_conv_b,
                   ln_b_g, ln_b_b, w_out_b, w_pool, pool_scale, w_o):
    bsz, s, _ = h.shape
    z = jnp.einsum('bsd,dn->bsn', h, w_in)
    bg, cg, ha, ga, gb, pin, gates = jnp.split(z, SPLITS, axis=-1)
    u = cg * ha
    full_a = jnp.concatenate([prev_a, u], axis=1)
    y_a = jnp.einsum('bsc,cd->bsd', bg * causal_depthwise(full_a, w_conv_a), w_out_a)
    v = ga * jax.nn.sigmoid(gb)
    full_b = jnp.concatenate([prev_b, v], axis=1)
    cb = causal_depthwise(full_b, w_conv_b) + b_conv_b
    y_b = jnp.einsum('bsc,cd->bsd', jax.nn.silu(layer_norm(cb, ln_b_g, ln_b_b)), w_out_b)
    full_p = jnp.concatenate([prev_p, pin], axis=1)
    pooled = multiscale_pool(full_p, pos0)
    y_c = jnp.einsum('bsgc,gce->bsge', pooled, w_pool).reshape(bsz, s, W_C) * pool_scale
    gt = jax.nn.sigmoid(gates).reshape(bsz, s, N_BRANCH, D_MODEL)
    m = gt[:, :, 0] * y_a + gt[:, :, 1] * y_b + gt[:, :, 2] * y_c
    out = jnp.einsum('bsd,de->bse', m, w_o)
    return (out, full_a[:, full_a.shape[1] - (K_A - 1):], full_b[:, full_b.shape[1] - (K_B - 1):],
            full_p[:, full_p.shape[1] - POOL_PAD:])


def run_trunk(x, c, prev_a, prev_b, prev_p, pos0, w_ada, b_ada, norm1_g, w_in, w_conv_a, w_out_a,
              w_conv_b, b_conv_b, ln_b_g, ln_b_b, w_out_b, w_pool, pool_scale, w_o, norm2_g,
              w_ffn_in, w_ffn_out, final_g):
    new_a, new_b, new_p = [], [], []
    sc = jax.nn.silu(c)
    for l in range(DEPTH):
        ada = jnp.einsum('bd,de->be', sc, w_ada[l]) + b_ada[l]
        sh1, s1, g1, sh2, s2, g2 = [t[:, None, :] for t in jnp.split(ada, 6, axis=-1)]
        h = rms_norm(x, norm1_g[l]) * (1 + s1) + sh1
        out, na, nb, npl = mixer_branches(h, prev_a[l], prev_b[l], prev_p[l], pos0, w_in[l], w_conv_a[l],
                                          w_out_a[l], w_conv_b[l], b_conv_b[l], ln_b_g[l], ln_b_b[l],
                                          w_out_b[l], w_pool[l], pool_scale[l], w_o[l])
        x = x + g1 * out
        h2 = rms_norm(x, norm2_g[l]) * (1 + s2) + sh2
        gate, up = jnp.split(jnp.einsum('bsd,df->bsf', h2, w_ffn_in[l]), 2, axis=-1)
        x = x + g2 * jnp.einsum('bsf,fd->bsd', jax.nn.silu(gate) * up, w_ffn_out[l])
        new_a.append(na); new_b.append(nb); new_p.append(npl)
    return rms_norm(x, final_g), jnp.stack(new_a), jnp.stack(new_b), jnp.stack(new_p)


def setup_inputs(seed: int = 0) -> dict:
    key = jax.random.key(seed)
    ks = jax.random.split(key, 26)
    L = DEPTH

    def n(k, shape, s):
        return jax.random.normal(k, shape, jnp.float32) * s

    return {
        "x_prompt": n(ks[0], (BATCH, SEQ, D_MODEL), 1.0),
        "x_sample": n(ks[1], (DEC_BATCH, DEC_SEQ, D_MODEL), 1.0),
        "c_prompt": n(ks[2], (BATCH, D_MODEL), 1.0),
        "c_sample": n(ks[3], (DEC_BATCH, D_MODEL), 1.0),
        "cache_conv_a": n(ks[4], (L, DEC_BATCH, K_A - 1, W_A), 1.0),
        "cache_conv_b": n(ks[5], (L, DEC_BATCH, K_B - 1, W_B), 0.5),
        "cache_pool": n(ks[6], (L, DEC_BATCH, POOL_PAD, W_C), 1.0),
        "w_ada": n(ks[7], (L, D_MODEL, 6 * D_MODEL), D_MODEL ** -0.5),
        "b_ada": n(ks[8], (L, 6 * D_MODEL), 0.01),
        "norm1_g": 1.0 + n(ks[9], (L, D_MODEL), 0.02),
        "w_in": n(ks[10], (L, D_MODEL, N_IN), D_MODEL ** -0.5),
        "w_conv_a": n(ks[11], (L, K_A, W_A), K_A ** -0.5),
        "w_out_a": n(ks[12], (L, W_A, D_MODEL), W_A ** -0.5),
        "w_conv_b": n(ks[13], (L, K_B, W_B), K_B ** -0.5),
        "b_conv_b": n(ks[14], (L, W_B), 0.01),
        "ln_b_g": 1.0 + n(ks[15], (L, W_B), 0.02),
        "ln_b_b": n(ks[16], (L, W_B), 0.01),
        "w_out_b": n(ks[17], (L, W_B, D_MODEL), W_B ** -0.5),
        "w_pool": n(ks[18], (L, N_POOL_GROUPS, POOL_GROUP, POOL_GROUP), POOL_GROUP ** -0.5),
        "pool_scale": 1.0 + n(ks[19], (L, W_C), 0.02),
        "w_o": n(ks[20], (L, D_MODEL, D_MODEL), D_MODEL ** -0.5),
        "norm2_g": 1.0 + n(ks[21], (L, D_MODEL), 0.02),
        "w_ffn_in": n(ks[22], (L, D_MODEL, 2 * D_FF), D_MODEL ** -0.5),
        "w_ffn_out": n(ks[23], (L, D_FF, D_MODEL), D_FF ** -0.5),
        "final_g": 1.0 + n(ks[24], (D_MODEL,), 0.02),
    }


def reference(x_prompt, x_sample, c_prompt, c_sample, cache_conv_a, cache_conv_b, cache_pool,
              w_ada, b_ada, norm1_g, w_in, w_conv_a, w_out_a, w_conv_b, b_conv_b, ln_b_g, ln_b_b,
              w_out_b, w_pool, pool_scale, w_o, norm2_g, w_ffn_in, w_ffn_out, final_g):
    bp = x_prompt.shape[0]
    dt = x_prompt.dtype
    zero_a = jnp.zeros((DEPTH, bp, K_A - 1, W_A), dt)
    zero_b = jnp.zeros((DEPTH, bp, K_B - 1, W_B), dt)
    zero_p = jnp.zeros((DEPTH, bp, POOL_PAD, W_C), dt)
    y_prompt, state_conv_a_prompt, state_conv_b_prompt, state_pool_prompt = run_trunk(
        x_prompt, c_prompt, zero_a, zero_b, zero_p, 0, w_ada, b_ada, norm1_g, w_in, w_conv_a, w_out_a,
        w_conv_b, b_conv_b, ln_b_g, ln_b_b, w_out_b, w_pool, pool_scale, w_o, norm2_g,
        w_ffn_in, w_ffn_out, final_g)
    y_sample, state_conv_a_sample, state_conv_b_sample, state_pool_sample = run_trunk(
        x_sample, c_sample, cache_conv_a, cache_conv_b, cache_pool, PAST_LEN, w_ada, b_ada, norm1_g, w_in,
        w_conv_a, w_out_a, w_conv_b, b_conv_b, ln_b_g, ln_b_b, w_out_b, w_pool, pool_scale, w_o, norm2_g,
        w_ffn_in, w_ffn_out, final_g)
    return (y_prompt, y_sample, state_conv_a_prompt, state_conv_b_prompt, state_pool_prompt,
            state_conv_a_sample, state_conv_b_sample, state_pool_sample)
```

```python
import numpy as np
import concourse.bass as bass
import concourse.mybir as mybir
from concourse.bass_utils import run_bass_kernel_spmd

F32 = mybir.dt.float32
BF16 = mybir.dt.bfloat16
AF = mybir.ActivationFunctionType
ALU = mybir.AluOpType

D = 1024
KC = 8
TP = 1024
TS = 64
T = TP + TS
NPASS = 4
NL = 2
DFF = 2816
NFB = DFF // 128
NIN = 9216
EPS = 1e-6
NCORES = 8
HR = 47
SUBT = [(0, 512, 0), (512, 512, 0), (1024, 64, 1)]
POOLW = (2, 4, 8, 16)
TD = 12

U_CB = 0
U_A2 = 48
U_B2 = 64
U_C2 = 80
U_O = 96
U_FI = 104
U_FO = 148
U_CV = 172
U_PER_LAYER = 204

V_N1 = 0
V_N2 = 8
V_CA = 16
V_CB = 40
V_BB = 288
V_LG = 296
V_LB = 304
V_PS = 312
V_BA = 320
V_PER_LAYER = 368
V_FG = 2 * V_PER_LAYER
V_ROWS = 768


class View:
    __slots__ = ("ap", "root", "ivs")

    def __init__(self, ap, root, ivs):
        self.ap = ap
        self.root = root
        self.ivs = ivs


class Buf:
    def __init__(self, space, root, tensor_ap, byte_off, dtype, K, C, nparts=128):
        self.space = space
        self.root = root
        self.K = K
        self.C = C
        self.es = 4 if dtype == F32 else 2
        self.byte_off = byte_off
        self.rs = C * self.es
        nb = K * C * self.es
        assert byte_off % 4 == 0 and nb % 4 == 0
        flat = tensor_ap[:, byte_off // 4:(byte_off + nb) // 4]
        if dtype != F32:
            flat = flat.bitcast(dtype)
        self.ap2 = flat
        self.ap3 = flat.rearrange("p (k c) -> p k c", k=K) if K > 1 else None
        self.nbytes = nb

    def v(self, k0=None, k1=None, c0=0, c1=None, p0=0, p1=128):
        if c1 is None:
            c1 = self.C
        if self.K == 1:
            ap = self.ap2[p0:p1, c0:c1]
            ivs = [(self.byte_off + c0 * self.es, self.byte_off + c1 * self.es)]
            return View(ap, (self.space, self.root), ivs)
        if k0 is None:
            k0, k1 = 0, self.K
        if k1 is None:
            ap = self.ap3[p0:p1, k0, c0:c1]
            ks = [k0]
        else:
            ap = self.ap3[p0:p1, k0:k1, c0:c1]
            ks = list(range(k0, k1))
        if c0 == 0 and c1 == self.C:
            ivs = [(self.byte_off + ks[0] * self.rs, self.byte_off + (ks[-1] + 1) * self.rs)]
        else:
            ivs = [(self.byte_off + k * self.rs + c0 * self.es, self.byte_off + k * self.rs + c1 * self.es) for k in ks]
        return View(ap, (self.space, self.root), ivs)


def _overlap(a, b):
    for (l0, h0) in a:
        for (l1, h1) in b:
            if l0 < h1 and l1 < h0:
                return True
    return False


def _covered(inner, outer):
    for (l0, h0) in inner:
        ok = False
        for (l1, h1) in outer:
            if l1 <= l0 and h0 <= h1:
                ok = True
                break
        if not ok:
            return False
    return True


class Op:
    __slots__ = ("eng", "emit", "deps", "sig", "sem", "ticket", "is_dma", "dkey", "ndma", "idx")


class Prog:
    COMPUTE = ("pe", "act", "dve", "pool")

    def __init__(self, same_engine_sync=True):
        self.ops = []
        self.acc = {}
        self.same_engine_sync = same_engine_sync

    def add(self, eng, emit, reads=(), writes=(), dkey=None, ndma=0, extra_deps=()):
        op = Op()
        op.eng = eng
        op.emit = emit
        op.is_dma = dkey is not None
        op.dkey = dkey
        op.ndma = ndma
        op.sig = op.is_dma
        op.idx = len(self.ops)
        deps = set(extra_deps)
        for r in reads:
            lst = self.acc.setdefault(r.root, [])
            for e in lst:
                if e[3] and _overlap(e[2], r.ivs):
                    deps.add(e[0])
            if not op.is_dma:
                lst[:] = [e for e in lst if e[0] == op.idx or not ((not e[3]) and e[1] == eng and e[2] == r.ivs and not self.ops[e[0]].is_dma)]
            lst.append([op.idx, eng, r.ivs, False])
        for w in writes:
            lst = self.acc.setdefault(w.root, [])
            keep = []
            for e in lst:
                if e[0] == op.idx:
                    keep.append(e)
                    continue
                if _overlap(e[2], w.ivs):
                    deps.add(e[0])
                    if _covered(e[2], w.ivs):
                        continue
                keep.append(e)
            keep.append([op.idx, eng, w.ivs, True])
            lst[:] = keep
        deps.discard(op.idx)
        op.deps = deps
        self.ops.append(op)
        return op.idx

    def finalize_and_emit(self, nc, block_engines, sems, dsems):
        ops = self.ops
        for op in ops:
            for d in op.deps:
                x = ops[d]
                if x.is_dma:
                    continue
                if x.eng == op.eng and not op.is_dma:
                    if op.eng == "pe" or not self.same_engine_sync:
                        continue
                x.sig = True
        cnt = {e: 0 for e in self.COMPUTE}
        dcnt = {}
        for op in ops:
            if op.is_dma:
                dcnt[op.dkey] = dcnt.get(op.dkey, 0) + 16 * op.ndma
                op.sem = dsems[op.dkey]
                op.ticket = dcnt[op.dkey]
            elif op.sig:
                cnt[op.eng] += 1
                op.sem = sems[op.eng]
                op.ticket = cnt[op.eng]
        streams = {}
        for op in ops:
            streams.setdefault(op.eng, []).append(op)
        self.stats = {e: len(s) for e, s in streams.items()}
        self.sigcnt = cnt

        def make_stream(eng_name):
            def fn(e):
                waited = {}
                for op in streams.get(eng_name, []):
                    need = {}
                    for d in op.deps:
                        x = ops[d]
                        if not x.is_dma and x.eng == op.eng and not op.is_dma:
                            if op.eng == "pe" or not self.same_engine_sync:
                                continue
                        key = id(x.sem)
                        if key not in need or need[key][1] < x.ticket:
                            need[key] = (x.sem, x.ticket)
                    for key, (sem, val) in need.items():
                        if waited.get(key, 0) < val:
                            e.wait_ge(sem, val)
                            waited[key] = val
                    if op.emit is not None:
                        if op.is_dma:
                            op.emit(e, op.sem)
                        else:
                            ins = op.emit(e)
                            if op.sig:
                                ins.then_inc(op.sem, 1)
            return fn

        return make_stream


def build_program(npass=NPASS, nlayer=NL, same_engine_sync=True, wring_units=11):
    nc = bass.Bass("TRN2", target_bir_lowering=False)
    ntok_p = npass * TP
    xp = nc.dram_tensor("xp", [NPASS * TP, D], F32, kind="ExternalInput").ap()
    xs = nc.dram_tensor("xs", [NPASS, TS, D], F32, kind="ExternalInput").ap()
    cvec = nc.dram_tensor("cvec", [5, D], F32, kind="ExternalInput").ap()
    cache = nc.dram_tensor("cache", [NL, NPASS, HR, D], F32, kind="ExternalInput").ap()
    vecs = nc.dram_tensor("vecs", [V_ROWS, 128], F32, kind="ExternalInput").ap()
    wall = nc.dram_tensor("wall", [NL * U_PER_LAYER, 128, 1024], F32, kind="ExternalInput").ap()
    wada = nc.dram_tensor("wada", [NL * 48, 128, 1024], F32, kind="ExternalInput").ap()
    yp = nc.dram_tensor("yp", [NPASS * TP, D], F32, kind="ExternalOutput").ap()
    ys = nc.dram_tensor("ys", [NPASS, TS, D], F32, kind="ExternalOutput").ap()
    stp = nc.dram_tensor("stp", [NL, HR, D], F32, kind="ExternalOutput").ap()
    sts = nc.dram_tensor("sts", [NL, NPASS, HR, D], F32, kind="ExternalOutput").ap()

    P = Prog(same_engine_sync=same_engine_sync)

    off = [0]

    def alloc(nbytes):
        o = off[0]
        off[0] += (nbytes + 3) // 4 * 4
        return o

    o_X = alloc(KC * T * 4)
    o_H = alloc(KC * T * 2)
    o_PA = alloc(KC * T * 2)
    o_PC = alloc(KC * T * 2)
    o_BIG = alloc(KC * T * 4)
    o_PB = alloc(KC * T * 2)
    o_WR = alloc(wring_units * 2048)
    CU = 2 + TP + 2 + TS
    CV = 30 + TP + 30 + TS
    CP = 15 + TP + 15 + TS
    CT = 1152
    o_UF = alloc(CU * 4)
    o_VF = alloc(2 * CV * 2)
    o_PF = alloc(CP * 4)
    o_T1 = alloc(CT * 4)
    o_T2 = alloc(CT * 4)
    o_T3 = alloc(CT * 4)
    o_ST = alloc(3 * 512 * 4)
    o_VEC = alloc(V_ROWS * 4)
    o_ID = alloc(128 * 4)
    o_ONE = alloc(128 * 2)
    o_HP = alloc(NL * KC * HR * 4)
    o_HS = alloc(NL * KC * HR * 4)
    o_ADA = alloc(NL * 48 * 5 * 4)
    o_AA = alloc(NL * 2 * KC * 5 * 4)
    o_SC = alloc(KC * 5 * 2 + 16)
    o_IC = alloc(4 * 16 * 4)
    o_EPS = alloc(4)
    o_BGS = alloc(T * 4)
    arena_bytes = off[0]
    assert arena_bytes <= (nc.sbuf_top - nc.sbuf_base - 64), arena_bytes

    import contextlib
    es = contextlib.ExitStack()
    with es:
        arena_t = es.enter_context(nc.sbuf_tensor("arena", [128, arena_bytes // 4], F32))
        psum_t = es.enter_context(nc.psum_tensor("psum", [128, 8 * 512], F32))
        A = arena_t[:, :]
        PSAP = psum_t[:, :]

        def sb(root, o, dtype, K, C):
            return Buf("sb", root, A, o, dtype, K, C)

        X = sb("X", o_X, F32, KC, T)
        H = sb("H", o_H, BF16, KC, T)
        PA = sb("PA", o_PA, BF16, KC, T)
        PC = sb("PC", o_PC, BF16, KC, T)
        BIG = sb("BIGPB", o_BIG, F32, KC, T)
        PB = sb("BIGPB", o_PB, BF16, KC, T)
        ACTB = sb("BIGPB", o_BIG, BF16, NFB, T)
        WR = [sb("WR%d" % i, o_WR + i * 2048, BF16, KC, 128) for i in range(wring_units)]
        UF = sb("UF", o_UF, F32, 1, CU)
        VFB = [sb("VF%d" % i, o_VF + i * CV * 2, BF16, 1, CV) for i in range(2)]
        PF = sb("PF", o_PF, F32, 1, CP)
        SQ2 = sb("UVP", o_UF, BF16, KC, 512)
        assert o_VF == o_UF + CU * 4 and o_PF == o_VF + 2 * CV * 2 and CU * 4 + 2 * CV * 2 >= KC * 512 * 2
        OSTG = [sb("UVP", o_UF, F32, 1, 1024), sb("UVP", o_PF, F32, 1, 1024)]
        SQH = sb("H", o_H, BF16, KC, 512)
        for b_ in (UF, PF, VFB[0], VFB[1]):
            b_.root = "UVP"
        T1 = sb("T1", o_T1, F32, 1, CT)
        T2 = sb("T2", o_T2, F32, 1, CT)
        T3 = sb("T3", o_T3, F32, 1, CT)
        SQB = sb("T1", o_T1, BF16, KC, 512)
        STG = [sb("T1", o_T1, F32, 1, 1024), sb("T2", o_T2 + 0, F32, 1, 1024)]
        STAT = sb("STAT", o_ST, F32, 3, 512)
        VEC = sb("VEC", o_VEC, F32, 1, V_ROWS)
        IDENT = sb("ID", o_ID, F32, 1, 128)
        ONES = sb("ONE", o_ONE, BF16, 1, 128)
        HP = sb("HP", o_HP, F32, NL * KC, HR)
        HS = sb("HS", o_HS, F32, NL * KC, HR)
        ADA = sb("ADA", o_ADA, F32, NL * 48, 5)
        AA = sb("AA", o_AA, F32, NL * 2 * KC, 5)
        SC = sb("SC", o_SC, BF16, KC, 5)
        IC = sb("IC", o_IC, F32, 4, 16)
        EPSC = sb("EPSC", o_EPS, F32, 1, 1)
        BGS = sb("BGS", o_BGS, F32, 1, T)
        for b in (T1, T2, SQB, STG[0], STG[1]):
            b.root = "T12"
        assert CT * 4 >= 4096 and o_T2 == o_T1 + CT * 4
        assert 2 * CT * 4 >= KC * 512 * 2

        PSB = [Buf("ps", "PS%d" % i, PSAP, i * 2048, F32, 1, 512) for i in range(8)]
        psi = [0]

        def bank():
            b = PSB[psi[0] % 8]
            psi[0] += 1
            return b

        def mm(out, lhsT, rhs, start, stop):
            P.add("pe", lambda e, o=out.ap, l=lhsT.ap, r=rhs.ap, s=start, t=stop: e.matmul(o, l, r, start=s, stop=t),
                  reads=[lhsT, rhs], writes=[out])

        def tr(out, in_, ident):
            P.add("pe", lambda e, o=out.ap, i=in_.ap, d=ident.ap: e.transpose(o, i, d), reads=[in_, ident], writes=[out])

        def act(out, in_, func, scale=1.0, bias=0.0, eng="act"):
            rd = [in_]
            sc = scale
            bi = bias
            if isinstance(scale, View):
                rd.append(scale)
                sc = scale.ap
            if isinstance(bias, View):
                rd.append(bias)
                bi = bias.ap
            P.add("act", lambda e, o=out.ap, i=in_.ap, f=func, s=sc, b=bi: e.activation(o, i, f, bias=b, scale=s),
                  reads=rd, writes=[out])

        def tt(out, a, b, op, eng="dve"):
            P.add(eng, lambda e, o=out.ap, x=a.ap, y=b.ap, p=op: e.tensor_tensor(o, x, y, p), reads=[a, b], writes=[out])

        def ts(out, a, s1, s2, op0, op1=None, eng="dve"):
            rd = [a]
            v1, v2 = s1, s2
            if isinstance(s1, View):
                rd.append(s1)
                v1 = s1.ap
            if isinstance(s2, View):
                rd.append(s2)
                v2 = s2.ap
            if op1 is None:
                P.add(eng, lambda e, o=out.ap, x=a.ap, u=v1, p=op0: e.tensor_scalar(o, x, u, None, p), reads=rd, writes=[out])
            else:
                P.add(eng, lambda e, o=out.ap, x=a.ap, u=v1, w=v2, p=op0, q=op1: e.tensor_scalar(o, x, u, w, p, q),
                      reads=rd, writes=[out])

        def stt(out, in0, scalar, in1, op0, op1, eng="dve"):
            rd = [in0, in1]
            sv = scalar
            if isinstance(scalar, View):
                rd.append(scalar)
                sv = scalar.ap
            P.add(eng, lambda e, o=out.ap, x=in0.ap, s=sv, y=in1.ap, p=op0, q=op1: e.scalar_tensor_tensor(o, x, s, y, p, q),
                  reads=rd, writes=[out])

        def powm05(v):
            P.add("pool", lambda e, a=v.ap: e.tensor_single_scalar(a, a, -0.5, ALU.pow), reads=[v], writes=[v])

        def cp(out, in_, eng="dve"):
            P.add(eng, lambda e, o=out.ap, i=in_.ap: e.tensor_copy(o, i), reads=[in_], writes=[out])

        def dma(queue, dkey, pairs, reads=(), writes=()):
            def emit(e, sem, pairs=pairs):
                for (o, i) in pairs:
                    e.dma_start(out=o, in_=i).then_inc(sem, 16)
            return P.add(queue, emit, reads=reads, writes=writes, dkey=dkey, ndma=len(pairs))

        def vcol(row):
            return VEC.v(c0=row, c1=row + 1)

        wr_next = [0]

        def load_unit(src_ap_full, nk=8):
            slot = WR[wr_next[0] % wring_units]
            key = "WR%d" % (wr_next[0] % wring_units)
            wr_next[0] += 1
            dst = slot.v(0, nk)
            dma("pool", key, [(dst.ap, src_ap_full[:, 0:nk * 128].rearrange("p (k n) -> p k n", k=nk))], writes=[dst])
            return slot

        pending = []
        loaded = []

        class WStream:
            def __init__(self):
                self.sched = []
                self.issued = 0
                self.slots = {}

            def plan(self, src, nk=8):
                self.sched.append((src, nk))
                return len(self.sched) - 1

        DEFER_ADA1 = nlayer > 1

        def sched_for(p):
            lst = []
            for l in range(nlayer):
                base = l * U_PER_LAYER
                for cb in range(KC + 1):
                    if cb < KC:
                        for j in (0, 1, 2, 5):
                            lst.append(("wall", base + U_CB + cb * 6 + j, 8))
                    if cb >= 1:
                        for u in range(TD // 8, 4):
                            lst.append(("wall", base + U_CV + (cb - 1) * 4 + u, 8 if u < 3 else 7))
                    if cb < KC:
                        lst.append(("wall", base + U_CB + cb * 6 + 3, 8))
                        lst.append(("wall", base + U_CB + cb * 6 + 4, 8))
                        if DEFER_ADA1 and p == 0 and l == 0:
                            for i in range(6):
                                lst.append(("wada", 48 + cb * 6 + i, 8))
                for u in range(U_A2, U_CV):
                    nk = 8
                    if U_C2 <= u < U_O and (u - U_C2) % 2 == 0:
                        nk = 2
                    if U_FO <= u and (u - U_FO) % 3 == 2:
                        nk = 6
                    lst.append(("wall", base + u, nk))
            return lst

        wsched = []
        for p_ in range(npass):
            wsched += sched_for(p_)
        wpos = {"issued": 0, "used": 0, "slots": {}}
        total_units = [0]

        def w_prefetch(upto):
            while wpos["issued"] < upto and wpos["issued"] < total_units[0]:
                g = wpos["issued"]
                (tn, ui, nk) = wsched[g]
                wpos["slots"][g] = load_unit(wall[ui] if tn == "wall" else wada[ui], nk)
                wpos["issued"] += 1

        def w_take(n):
            g = wpos["used"]
            w_prefetch(g + wring_units)
            wpos["used"] += n
            return [wpos["slots"].pop(g + i) for i in range(n)]

        def w_next():
            return w_take(1)[0]

        total_units[0] = len(wsched)

        P.add("pool", lambda e: e.memset(IDENT.v().ap, 0.0), writes=[IDENT.v()])
        P.add("pool", lambda e: e.iota(IDENT.v().ap, [[1, 128]], channel_multiplier=-1, allow_small_or_imprecise_dtypes=True),
              writes=[IDENT.v()], reads=[IDENT.v()])
        P.add("pool", lambda e: e.tensor_single_scalar(IDENT.v().ap, IDENT.v().ap, 0.0, ALU.is_equal),
              reads=[IDENT.v()], writes=[IDENT.v()])
        P.add("pool", lambda e: e.memset(ONES.v().ap, 1.0), writes=[ONES.v()])
        P.add("pool", lambda e: e.memset(EPSC.v().ap, EPS), writes=[EPSC.v()])
        P.add("pool", lambda e: e.memset(HP.v().ap, 0.0), writes=[HP.v()])
        for g, w in enumerate(POOLW):
            for j in range(15):
                val = 1.0 / min(j + 1, w)
                P.add("pool", lambda e, a=IC.v(g, None, j, j + 1).ap, v=val: e.memset(a, v), writes=[IC.v(g, None, j, j + 1)])

        for i in range(V_ROWS // 128):
            s = STG[i % 2]
            dma("sp", "STG%d" % (i % 2), [(s.v(c0=0, c1=128).ap, vecs[i * 128:(i + 1) * 128, :])], writes=[s.v(c0=0, c1=128)])
            b = bank()
            tr(b.v(c0=0, c1=128), s.v(c0=0, c1=128), IDENT.v())
            cp(VEC.v(c0=i * 128, c1=(i + 1) * 128), b.v(c0=0, c1=128), eng="dve")
        s = STG[0]
        dma("sp", "STG0", [(s.v(c0=0, c1=1024, p1=5).ap, cvec[:, :])], writes=[s.v()])
        b = bank()
        for k in range(KC):
            tr(b.v(c0=k * 5, c1=k * 5 + 5), s.v(c0=k * 128, c1=(k + 1) * 128, p1=5), IDENT.v(c0=0, c1=5, p1=5))
        act(T3.v(c0=0, c1=40), b.v(c0=0, c1=40), AF.Sigmoid)
        P.add("dve", lambda e, o=SC.ap2[:, 0:40], x=T3.v(c0=0, c1=40).ap, y=b.v(c0=0, c1=40).ap: e.tensor_tensor(o, x, y, ALU.mult),
              reads=[T3.v(c0=0, c1=40), b.v(c0=0, c1=40)], writes=[SC.v()])

        def ada_evac(l, b, j0, nj):
            adal = Buf("sb", "ADA", A, o_ADA + l * 48 * 5 * 4, F32, 48, 5)
            r0 = l * V_PER_LAYER + V_BA + j0
            bview = View(VEC.ap2[:, r0:r0 + nj].unsqueeze(2).to_broadcast([128, nj, 5]), VEC.v().root, VEC.v().ivs)
            pview = View(b.ap2[:, 0:nj * 5].rearrange("p (j s) -> p j s", j=nj), b.v().root, b.v(c0=0, c1=nj * 5).ivs)
            tt(adal.v(j0, j0 + nj), pview, bview, ALU.add)

        def ada_derive(l):
            adal = Buf("sb", "ADA", A, o_ADA + l * 48 * 5 * 4, F32, 48, 5)
            for which, (vrow, aoff) in enumerate(((V_N1, 8), (V_N2, 32))):
                aab = Buf("sb", "AA", A, o_AA + (l * 2 + which) * KC * 5 * 4, F32, KC, 5)
                gview = View(VEC.ap2[:, l * V_PER_LAYER + vrow:l * V_PER_LAYER + vrow + 8].unsqueeze(2).to_broadcast([128, 8, 5]),
                             VEC.v().root, VEC.v().ivs)
                stt(aab.v(), adal.v(aoff, aoff + 8), 1.0, gview, ALU.add, ALU.mult)

        for l in range(1 if DEFER_ADA1 else nlayer):
            b = bank()
            for j in range(48):
                slot = WR[wr_next[0] % wring_units]
                key = "WR%d" % (wr_next[0] % wring_units)
                wr_next[0] += 1
                dst = slot.v(0, 8)
                dma("pool", key, [(dst.ap, wada[l * 48 + j].rearrange("p (k n) -> p k n", k=8))], writes=[dst])
                for k in range(KC):
                    mm(b.v(c0=j * 5, c1=j * 5 + 5), slot.v(k, None), SC.v(k, None), k == 0, k == KC - 1)
            ada_evac(l, b, 0, 48)
            ada_derive(l)

        def ada1_step(cb):
            us = w_take(6)
            b = bank()
            for i, slot in enumerate(us):
                for k in range(KC):
                    mm(b.v(c0=i * 5, c1=i * 5 + 5), slot.v(k, None), SC.v(k, None), k == 0, k == KC - 1)
            ada_evac(1, b, cb * 6, 6)
            if cb == KC - 1:
                ada_derive(1)

        def ada_col(l, j, s):
            adal = Buf("sb", "ADA", A, o_ADA + l * 48 * 5 * 4, F32, 48, 5)
            return adal.v(j, None, s, s + 1)

        def aa_col(l, which, k, s):
            aab = Buf("sb", "AA", A, o_AA + (l * 2 + which) * KC * 5 * 4, F32, KC, 5)
            return aab.v(k, None, s, s + 1)

        def hist(HB, l, cb, c0, c1):
            return HB.v(l * KC + cb, None, c0, c1)

        def rms_all(sqb, presq=False):
            rs = []
            for si, (c0, n, sq_i) in enumerate(SUBT):
                b = bank()
                for k in range(KC):
                    if presq:
                        sq = sqb.v(k, None, c0, c0 + n)
                    else:
                        sq = sqb.v(k, None, 0, n)
                        xv = X.v(k, None, c0, c0 + n)
                        if k % 2 == 0:
                            act(sq, xv, AF.Square)
                        else:
                            tt(sq, xv, xv, ALU.mult)
                    mm(b.v(c0=0, c1=n), ONES.v(), sq, k == 0, k == KC - 1)
                r = STAT.v(si, None, 0, n)
                act(r, b.v(c0=0, c1=n), AF.Sqrt, scale=1.0 / D, bias=EPSC.v())
                P.add("dve", lambda e, a=r.ap: e.reciprocal(a, a), reads=[r], writes=[r])
                rs.append(r)
            return rs

        def ada_norm(l, which, seqslots, presq=False):
            shoff = 0 if which == 0 else 24
            rs = rms_all(PA, True) if presq else rms_all(SQB)
            for si, (c0, n, sq_i) in enumerate(SUBT):
                s = seqslots[sq_i]
                for k in range(KC):
                    tmp = T3.v(c0=(k % 2) * 512, c1=(k % 2) * 512 + n)
                    tt(tmp, X.v(k, None, c0, c0 + n), rs[si], ALU.mult)
                    act(H.v(k, None, c0, c0 + n), tmp, AF.Identity, scale=aa_col(l, which, k, s), bias=ada_col(l, shoff + k, s))

        def norm_st(l, which, seqslots, si, sqb, part="ABC"):
            shoff = 0 if which == 0 else 24
            (c0, n, sq_i) = SUBT[si]
            if "A" in part:
                for k in range(KC):
                    sq = sqb.v(k, None, 0, n)
                    xv = X.v(k, None, c0, c0 + n)
                    if k % 2 == 0:
                        act(sq, xv, AF.Square)
                    else:
                        tt(sq, xv, xv, ALU.mult)
            r = STAT.v(si, None, 0, n)
            if "B" in part:
                b = bank()
                for k in range(KC):
                    mm(b.v(c0=0, c1=n), ONES.v(), sqb.v(k, None, 0, n), k == 0, k == KC - 1)
                act(r, b.v(c0=0, c1=n), AF.Sqrt, scale=1.0 / D, bias=EPSC.v())
                P.add("dve", lambda e, a=r.ap: e.reciprocal(a, a), reads=[r], writes=[r])
            if "C" in part:
                s_ = seqslots[sq_i]
                for k in range(KC):
                    tmp = T3.v(c0=(k % 2) * 512, c1=(k % 2) * 512 + n)
                    tt(tmp, X.v(k, None, c0, c0 + n), r, ALU.mult)
                    act(H.v(k, None, c0, c0 + n), tmp, AF.Identity, scale=aa_col(l, which, k, s_), bias=ada_col(l, shoff + k, s_))

        def conv_b(l, cbp):
            vbq = l * V_PER_LAYER
            u0 = TD // 8
            dv = w_take(4 - u0)
            vfbp = VFB[cbp % 2]
            for (c0, n, sq_i) in SUBT:
                bk = bank()
                off = c0 if sq_i == 0 else TP + 30 + (c0 - TP)
                for j in range(TD, 31):
                    mm(bk.v(c0=0, c1=n), dv[j // 8 - u0].v(j % 8, None), vfbp.v(c0=off + j, c1=off + j + n), j == TD, j == 30)
                bv = BIG.v(cbp, None, c0, c0 + n)
                stt(bv, bk.v(c0=0, c1=n), vcol(vbq + V_BB + cbp), bv, ALU.add, ALU.add)

        def layer(p, l, seqslots, presq=False):
            vb = l * V_PER_LAYER
            first_prompt = (p == 0)
            deferred = []

            def drain(n=1):
                for _ in range(min(n, len(deferred))):
                    deferred.pop(0)()

            ada_norm(l, 0, seqslots, presq)
            for cb in range(KC):
                wbg, wcg, wha, wpin = w_take(4)
                bgb = []
                for (c0, n, sq_i) in SUBT:
                    bc, bh, bb = bank(), bank(), bank()
                    for k in range(KC):
                        mm(bc.v(c0=0, c1=n), wcg.v(k, None), H.v(k, None, c0, c0 + n), k == 0, k == KC - 1)
                    for k in range(KC):
                        mm(bh.v(c0=0, c1=n), wha.v(k, None), H.v(k, None, c0, c0 + n), k == 0, k == KC - 1)
                    for k in range(KC):
                        mm(bb.v(c0=0, c1=n), wbg.v(k, None), H.v(k, None, c0, c0 + n), k == 0, k == KC - 1)
                    act(T3.v(c0=c0, c1=c0 + n), bc.v(c0=0, c1=n), AF.Copy)
                    act(BGS.v(c0=c0, c1=c0 + n), bb.v(c0=0, c1=n), AF.Copy)
                    uoff = 2 + c0 if sq_i == 0 else 2 + TP + 2 + (c0 - TP)
                    tt(UF.v(c0=uoff, c1=uoff + n), bh.v(c0=0, c1=n), T3.v(c0=c0, c1=c0 + n), ALU.mult)
                    drain(1)
                cp(UF.v(c0=0, c1=2), hist(HP, l, cb, 0, 2), eng="pool")
                cp(UF.v(c0=2 + TP, c1=4 + TP), hist(HS, l, cb, 0, 2), eng="pool")
                NO = TP + 2 + TS
                wa = [vcol(vb + V_CA + j * 8 + cb) for j in range(3)]
                ts(T2.v(c0=0, c1=NO), UF.v(c0=0, c1=NO), wa[0], None, ALU.mult)
                stt(T2.v(c0=0, c1=NO), UF.v(c0=1, c1=1 + NO), wa[1], T2.v(c0=0, c1=NO), ALU.mult, ALU.add)
                stt(T2.v(c0=0, c1=NO), UF.v(c0=2, c1=2 + NO), wa[2], T2.v(c0=0, c1=NO), ALU.mult, ALU.add)
                drain(1)
                for i, (c0, n, sq_i) in enumerate(SUBT):
                    toff = c0 if sq_i == 0 else TP + 2 + (c0 - TP)
                    tt(PA.v(cb, None, c0, c0 + n), T2.v(c0=toff, c1=toff + n), BGS.v(c0=c0, c1=c0 + n), ALU.mult)
                    drain(1)
                cp(hist(HP, l, cb, 0, 2), UF.v(c0=TP, c1=TP + 2), eng="pool")
                cp(hist(HS, l, cb, 0, 2), UF.v(c0=CU - 2, c1=CU), eng="pool")
                for (c0, n, sq_i) in SUBT:
                    bp = bank()
                    for k in range(KC):
                        mm(bp.v(c0=0, c1=n), wpin.v(k, None), H.v(k, None, c0, c0 + n), k == 0, k == KC - 1)
                    poff = 15 + c0 if sq_i == 0 else 15 + TP + 15 + (c0 - TP)
                    act(PF.v(c0=poff, c1=poff + n), bp.v(c0=0, c1=n), AF.Copy)
                cp(PF.v(c0=0, c1=15), hist(HP, l, cb, 32, 47), eng="pool")
                cp(PF.v(c0=15 + TP, c1=30 + TP), hist(HS, l, cb, 32, 47), eng="pool")
                g = cb // 2
                w = POOLW[g]
                src = PF
                sh = 1
                tbufs = [T2, T3]
                ti = 0
                while sh < w:
                    dst = tbufs[ti % 2]
                    ti += 1
                    tt(dst.v(c0=sh, c1=CP), src.v(c0=sh, c1=CP), src.v(c0=0, c1=CP - sh), ALU.add)
                    src = dst
                    sh *= 2
                for (c0, n, sq_i) in SUBT:
                    poff = 15 + c0 if sq_i == 0 else 15 + TP + 15 + (c0 - TP)
                    stt(PC.v(cb, None, c0, c0 + n), src.v(c0=poff, c1=poff + n), 1.0 / w, PF.v(c0=poff, c1=poff + n),
                        ALU.mult, ALU.subtract)
                    drain(1)
                if first_prompt:
                    tmpv = UF.v(c0=0, c1=15)
                    tt(tmpv, src.v(c0=15, c1=30), IC.v(g, None, 0, 15), ALU.mult)
                    tt(PC.v(cb, None, 0, 15), tmpv, PF.v(c0=15, c1=30), ALU.subtract)
                cp(hist(HP, l, cb, 32, 47), PF.v(c0=TP, c1=TP + 15), eng="pool")
                cp(hist(HS, l, cb, 32, 47), PF.v(c0=CP - 15, c1=CP), eng="pool")
                vfb = VFB[cb % 2]
                drain(100)
                if cb >= 1:
                    conv_b(l, cb - 1)
                wga, wgb = w_take(2)
                cp(T1.v(c0=0, c1=30), hist(HP, l, cb, 2, 32), eng="pool")
                cp(T1.v(c0=30 + TP, c1=60 + TP), hist(HS, l, cb, 2, 32), eng="pool")
                for (c0, n, sq_i) in SUBT:
                    bgt, bga = bank(), bank()
                    for k in range(KC):
                        mm(bgt.v(c0=0, c1=n), wgb.v(k, None), H.v(k, None, c0, c0 + n), k == 0, k == KC - 1)
                    for k in range(KC):
                        mm(bga.v(c0=0, c1=n), wga.v(k, None), H.v(k, None, c0, c0 + n), k == 0, k == KC - 1)
                    voff = 30 + c0 if sq_i == 0 else 30 + TP + 30 + (c0 - TP)
                    vv = T1.v(c0=voff, c1=voff + n)
                    act(vv, bgt.v(c0=0, c1=n), AF.Sigmoid)
                    tt(vv, bga.v(c0=0, c1=n), vv, ALU.mult)
                act(vfb.v(c0=0, c1=CV), T1.v(c0=0, c1=CV), AF.Copy)
                cp(hist(HP, l, cb, 2, 32), T1.v(c0=TP, c1=TP + 30), eng="pool")
                cp(hist(HS, l, cb, 2, 32), T1.v(c0=CV - 30, c1=CV), eng="pool")
                wbv = [vcol(vb + V_CB + j * 8 + cb) for j in range(31)]
                bp_, bs_ = BIG.v(cb, None, 0, TP), BIG.v(cb, None, TP, T)
                so = TP + 30
                def tap0(bp_=bp_, bs_=bs_, w0=wbv[0]):
                    ts(bp_, T1.v(c0=0, c1=TP), w0, None, ALU.mult)
                    ts(bs_, T1.v(c0=so, c1=so + TS), w0, None, ALU.mult)
                deferred.append(tap0)
                for j in range(1, TD):
                    def tapj(j=j, bp_=bp_, bs_=bs_, wj=wbv[j]):
                        stt(bp_, T1.v(c0=j, c1=j + TP), wj, bp_, ALU.mult, ALU.add)
                        stt(bs_, T1.v(c0=so + j, c1=so + j + TS), wj, bs_, ALU.mult, ALU.add)
                    deferred.append(tapj)
                drain(2)
                if DEFER_ADA1 and p == 0 and l == 0:
                    ada1_step(cb)
            drain(100)
            conv_b(l, KC - 1)
            lnst = []
            def srow(si, n):
                if si == 0:
                    return STAT.v(0, None, 0, n), STAT.v(1, None, 0, n)
                if si == 1:
                    return STAT.v(2, None, 0, n), T3.v(c0=0, c1=n)
                return T3.v(c0=512, c1=512 + n), T3.v(c0=576, c1=576 + n)
            for si, (c0, n, sq_i) in enumerate(SUBT):
                b1, b2 = bank(), bank()
                for k in range(KC):
                    cbf = SQB.v(k, None, 0, n)
                    cp(cbf, BIG.v(k, None, c0, c0 + n))
                    mm(b1.v(c0=0, c1=n), ONES.v(), cbf, k == 0, k == KC - 1)
                for k in range(KC):
                    sq = SQ2.v(k, None, 0, n)
                    act(sq, BIG.v(k, None, c0, c0 + n), AF.Square)
                    mm(b2.v(c0=0, c1=n), ONES.v(), sq, k == 0, k == KC - 1)
                mean, rstd = srow(si, n)
                ts(mean, b1.v(c0=0, c1=n), 1.0 / D, None, ALU.mult)
                stt(rstd, mean, -1.0, mean, ALU.mult, ALU.mult)
                stt(rstd, b2.v(c0=0, c1=n), 1.0 / D, rstd, ALU.mult, ALU.add)
                act(rstd, rstd, AF.Sqrt, scale=1.0, bias=EPSC.v())
                P.add("dve", lambda e, a=rstd.ap: e.reciprocal(a, a), reads=[rstd], writes=[rstd])
            def ln_norm(k):
                for si, (c0, n, sq_i) in enumerate(SUBT):
                    mean, rstd = srow(si, n)
                    bv = BIG.v(k, None, c0, c0 + n)
                    tt(bv, bv, mean, ALU.subtract)
                    tt(bv, bv, rstd, ALU.mult)
                    act(PB.v(k, None, c0, c0 + n), bv, AF.Silu, scale=vcol(vb + V_LG + k), bias=vcol(vb + V_LB + k))
            for br in range(3):
                for ob in range(KC):
                    wy, wg = w_take(2)
                    if br == 0:
                        ln_norm(ob)
                    for (c0, n, sq_i) in SUBT:
                        by, bg_ = bank(), bank()
                        if br == 0:
                            for k in range(KC):
                                mm(by.v(c0=0, c1=n), wy.v(k, None), PA.v(k, None, c0, c0 + n), k == 0, k == KC - 1)
                        elif br == 1:
                            for k in range(KC):
                                mm(by.v(c0=0, c1=n), wy.v(k, None), PB.v(k, None, c0, c0 + n), k == 0, k == KC - 1)
                        else:
                            gq = ob // 2
                            for j in range(2):
                                mm(by.v(c0=0, c1=n), wy.v(j, None), PC.v(2 * gq + j, None, c0, c0 + n), j == 0, j == 1)
                        for k in range(KC):
                            mm(bg_.v(c0=0, c1=n), wg.v(k, None), H.v(k, None, c0, c0 + n), k == 0, k == KC - 1)
                        sg = T1.v(c0=c0, c1=c0 + n)
                        act(sg, bg_.v(c0=0, c1=n), AF.Sigmoid)
                        mv = BIG.v(ob, None, c0, c0 + n)
                        if br == 0:
                            tt(mv, by.v(c0=0, c1=n), sg, ALU.mult)
                        elif br == 1:
                            tmp = T2.v(c0=c0, c1=c0 + n)
                            tt(tmp, by.v(c0=0, c1=n), sg, ALU.mult)
                            tt(mv, mv, tmp, ALU.add)
                        else:
                            tmp = T2.v(c0=c0, c1=c0 + n)
                            stt(tmp, by.v(c0=0, c1=n), vcol(vb + V_PS + ob), sg, ALU.mult, ALU.mult)
                            tt(PA.v(ob, None, c0, c0 + n), mv, tmp, ALU.add)
            wos = w_take(8)
            def o_st(si):
                (c0, n, sq_i) = SUBT[si]
                for ob in range(KC):
                    bo = bank()
                    for k in range(KC):
                        mm(bo.v(c0=0, c1=n), wos[ob].v(k, None), PA.v(k, None, c0, c0 + n), k == 0, k == KC - 1)
                    xv = X.v(ob, None, c0, c0 + n)
                    stt(xv, bo.v(c0=0, c1=n), ada_col(l, 16 + ob, seqslots[sq_i]), xv, ALU.mult, ALU.add)

            SQ3 = sb("PC", o_PC, BF16, KC, 512)
            o_st(0)
            norm_st(l, 1, seqslots, 0, SQ2, "A")
            o_st(2)
            norm_st(l, 1, seqslots, 0, SQ2, "BC")
            norm_st(l, 1, seqslots, 2, SQ3, "A")
            o_st(1)
            norm_st(l, 1, seqslots, 2, SQ3, "BC")
            norm_st(l, 1, seqslots, 1, SQ2, "ABC")

            def ffn_in(fb, si, wgt, wup):
                (c0, n, sq_i) = SUBT[si]
                bgt, bup = bank(), bank()
                for k in range(KC):
                    mm(bgt.v(c0=0, c1=n), wgt.v(k, None), H.v(k, None, c0, c0 + n), k == 0, k == KC - 1)
                for k in range(KC):
                    mm(bup.v(c0=0, c1=n), wup.v(k, None), H.v(k, None, c0, c0 + n), k == 0, k == KC - 1)
                gs = T1.v(c0=c0, c1=c0 + n)
                act(gs, bgt.v(c0=0, c1=n), AF.Silu)
                tt(ACTB.v(fb, None, c0, c0 + n), bup.v(c0=0, c1=n), gs, ALU.mult)

            NG = 3
            w6 = w_take(2 * NG)
            for grp in ((0, 2), (1,)):
                for j in range(NG):
                    for si in grp:
                        ffn_in(j, si, w6[2 * j], w6[2 * j + 1])
            for fb in range(NG, NFB):
                wgt, wup = w_take(2)
                for si in range(3):
                    ffn_in(fb, si, wgt, wup)
            for ob in range(KC):
                wf = w_take(3)
                for (c0, n, sq_i) in SUBT:
                    bo = bank()
                    for f in range(NFB):
                        mm(bo.v(c0=0, c1=n), wf[f // 8].v(f % 8, None), ACTB.v(f, None, c0, c0 + n), f == 0, f == NFB - 1)
                    xv = X.v(ob, None, c0, c0 + n)
                    stt(xv, bo.v(c0=0, c1=n), ada_col(l, 40 + ob, seqslots[sq_i]), xv, ALU.mult, ALU.add)
                    if (ob + (0 if sq_i == 0 else 1)) % 2 == 0:
                        act(PA.v(ob, None, c0, c0 + n), xv, AF.Square)
                    else:
                        tt(PA.v(ob, None, c0, c0 + n), xv, xv, ALU.mult)

        store_ops = []
        ostg_i = [0]

        def next_ostg():
            i = ostg_i[0] % 2
            ostg_i[0] += 1
            return OSTG[i], "OSTG%d" % i

        def pass_tiles(p):
            tl = [(xp[p * TP + i * 128:p * TP + (i + 1) * 128, :], 128, i * 128) for i in range(TP // 128)]
            tl.append((xs[p], TS, TP))
            return tl

        def load_dma(p, i):
            (src, nt, c0) = pass_tiles(p)[i]
            s_ = STG[i % 2]
            dma("sp", "STG%d" % (i % 2), [(s_.v(c0=0, c1=1024, p1=nt).ap, src)], writes=[s_.v()])

        def load_xpose(p, i):
            (src, nt, c0) = pass_tiles(p)[i]
            s_ = STG[i % 2]
            for half in range(2):
                b = bank()
                for kk in range(4):
                    k = half * 4 + kk
                    tr(b.v(c0=kk * 128, c1=kk * 128 + nt), s_.v(c0=k * 128, c1=(k + 1) * 128, p1=nt),
                       IDENT.v(c0=0, c1=nt, p1=nt))
                src_v = View(b.ap2[:, 0:512].rearrange("p (k n) -> p k n", k=4)[:, :, 0:nt], b.v().root, b.v().ivs)
                P.add("act", lambda e, o=X.v(half * 4, half * 4 + 4, c0, c0 + nt).ap, i_=src_v.ap: e.copy(o, i_),
                      reads=[src_v], writes=[X.v(half * 4, half * 4 + 4, c0, c0 + nt)])

        def load_caches(p):
            for l in range(nlayer):
                s_ = STG[l % 2]
                dma("sp", "STG%d" % (l % 2), [(s_.v(c0=0, c1=1024, p1=HR).ap, cache[l, p])], writes=[s_.v()])
                b = bank()
                for k in range(KC):
                    tr(b.v(c0=k * HR, c1=(k + 1) * HR), s_.v(c0=k * 128, c1=(k + 1) * 128, p1=HR), IDENT.v(c0=0, c1=HR, p1=HR))
                src_v = View(b.ap2[:, 0:KC * HR].rearrange("p (k n) -> p k n", k=KC), b.v().root, b.v(c0=0, c1=KC * HR).ivs)
                P.add("act", lambda e, o=HS.v(l * KC, (l + 1) * KC).ap, i_=src_v.ap: e.copy(o, i_),
                      reads=[src_v], writes=[HS.v(l * KC, (l + 1) * KC)])

        def out_tile(p, i, rs):
            (src, nt, c0) = pass_tiles(p)[i]
            si = 0 if c0 < 512 else (1 if c0 < TP else 2)
            t0 = c0 - SUBT[si][0]
            yt = Buf("sb", "T3", A, o_T3, F32, KC, 128)
            for k in range(KC):
                stt(yt.v(k, None, 0, nt), X.v(k, None, c0, c0 + nt), vcol(V_FG + k),
                    STAT.v(si, None, t0, t0 + nt), ALU.mult, ALU.mult)
            s_, key = next_ostg()
            for half in range(2):
                b = bank()
                for kk in range(4):
                    k = half * 4 + kk
                    tr(b.v(c0=kk * 128, c1=(kk + 1) * 128, p1=nt), yt.v(k, None, 0, nt), IDENT.v())
                P.add("act", lambda e, o=s_.v(c0=half * 512, c1=half * 512 + 512, p1=nt).ap, i_=b.v(p1=nt).ap: e.copy(o, i_),
                      reads=[b.v()], writes=[s_.v(c0=half * 512, c1=half * 512 + 512)])
            if c0 < TP:
                dst = yp[p * TP + c0:p * TP + c0 + nt, :]
            else:
                dst = ys[p, 0:nt, :]
            store_ops.append(dma("sp", key, [(dst, s_.v(c0=0, c1=1024, p1=nt).ap)], reads=[s_.v()]))

        NT_ = TP // 128 + 1
        for i in range(NT_):
            load_dma(0, i)
            load_xpose(0, i)
        load_caches(0)
        for p in range(npass):
            seqslots = (0, 1 + p)
            for l in range(nlayer):
                layer(p, l, seqslots, presq=(l > 0))
            rs = rms_all(PA, True)
            nxt = p + 1 < npass
            if nxt:
                load_dma(p + 1, 0)
            for i in range(NT_):
                if nxt and i + 1 < NT_:
                    load_dma(p + 1, i + 1)
                out_tile(p, i, rs)
                if nxt:
                    load_xpose(p + 1, i)
            outs = [(HS, l, sts[l, p]) for l in range(nlayer)]
            if p == npass - 1:
                outs += [(HP, l, stp[l]) for l in range(nlayer)]
            for (HB, l, dst) in outs:
                s_, key = next_ostg()
                for half in range(2):
                    b = bank()
                    for kk in range(4):
                        k = half * 4 + kk
                        tr(b.v(c0=kk * 128, c1=(kk + 1) * 128, p1=HR), HB.v(l * KC + k, None), IDENT.v())
                    P.add("act", lambda e, o=s_.v(c0=half * 512, c1=half * 512 + 512, p1=HR).ap, i_=b.v(p1=HR).ap: e.copy(o, i_),
                          reads=[b.v()], writes=[s_.v(c0=half * 512, c1=half * 512 + 512)])
                store_ops.append(dma("sp", key, [(dst, s_.v(c0=0, c1=1024, p1=HR).ap)], reads=[s_.v()]))
            if nxt:
                load_caches(p + 1)

        P.add("sp", None, extra_deps=store_ops)

        sem_names = ["pe", "act", "dve", "pool"]
        dkeys = ["WR%d" % i for i in range(wring_units)] + ["STG0", "STG1", "OSTG0", "OSTG1"]
        sems = {}
        dsems = {}
        for n_ in sem_names:
            sems[n_] = es.enter_context(nc.semaphore("s_" + n_))
        for k_ in dkeys:
            dsems[k_] = es.enter_context(nc.semaphore("d_" + k_))
        block = es.enter_context(nc.Block())
        mk = P.finalize_and_emit(nc, None, sems, dsems)
        block.sync(mk("sp"))
        block.gpsimd(mk("pool"))
        block.tensor(mk("pe"))
        block.scalar(mk("act"))
        block.vector(mk("dve"))
    return nc, P


def _unit(W, c0, r0=0, nk=8):
    blk = W[r0:r0 + nk * 128, c0:c0 + 128].reshape(nk, 128, 128).transpose(1, 0, 2).reshape(128, nk * 128)
    return blk


def prep_shared(inp):
    wall = np.zeros((NL * U_PER_LAYER, 128, 1024), np.float32)
    wada = np.zeros((NL * 48, 128, 1024), np.float32)
    vecs = np.zeros((V_ROWS, 128), np.float32)
    for l in range(NL):
        base = l * U_PER_LAYER
        w_in = inp["w_in"][l]
        splits = {"bg": 0, "cg": 1, "ha": 2, "ga": 3, "gb": 4, "pin": 5}
        for cb in range(KC):
            for j, nm in enumerate(("bg", "cg", "ha", "ga", "gb", "pin")):
                wall[base + U_CB + cb * 6 + j] = _unit(w_in, splits[nm] * 1024 + cb * 128)
        for ob in range(KC):
            wall[base + U_A2 + ob * 2] = _unit(inp["w_out_a"][l], ob * 128)
            wall[base + U_A2 + ob * 2 + 1] = _unit(w_in, 6 * 1024 + ob * 128)
            wall[base + U_B2 + ob * 2] = _unit(inp["w_out_b"][l], ob * 128)
            wall[base + U_B2 + ob * 2 + 1] = _unit(w_in, 7 * 1024 + ob * 128)
            g = ob // 2
            wall[base + U_C2 + ob * 2, :, 0:256] = _unit(inp["w_pool"][l, g], (ob % 2) * 128, nk=2)
            wall[base + U_C2 + ob * 2 + 1] = _unit(w_in, 8 * 1024 + ob * 128)
            wall[base + U_O + ob] = _unit(inp["w_o"][l], ob * 128)
            wfo = inp["w_ffn_out"][l]
            wall[base + U_FO + ob * 3] = _unit(wfo, ob * 128, 0, 8)
            wall[base + U_FO + ob * 3 + 1] = _unit(wfo, ob * 128, 1024, 8)
            wall[base + U_FO + ob * 3 + 2, :, 0:768] = _unit(wfo, ob * 128, 2048, 6)
        wcb = inp["w_conv_b"][l]
        ar = np.arange(128)
        for cb in range(KC):
            for j in range(31):
                u = base + U_CV + cb * 4 + j // 8
                blk = wall[u].reshape(128, 8, 128)
                blk[ar, j % 8, ar] = wcb[j, cb * 128:(cb + 1) * 128]
        wfi = inp["w_ffn_in"][l]
        for fb in range(NFB):
            wall[base + U_FI + fb * 2] = _unit(wfi, fb * 128)
            wall[base + U_FI + fb * 2 + 1] = _unit(wfi, DFF + fb * 128)
        for j in range(48):
            wada[l * 48 + j] = _unit(inp["w_ada"][l], j * 128)
        vb = l * V_PER_LAYER
        vecs[vb + V_N1:vb + V_N1 + 8] = inp["norm1_g"][l].reshape(8, 128)
        vecs[vb + V_N2:vb + V_N2 + 8] = inp["norm2_g"][l].reshape(8, 128)
        vecs[vb + V_CA:vb + V_CA + 24] = inp["w_conv_a"][l].reshape(24, 128)
        vecs[vb + V_CB:vb + V_CB + 248] = inp["w_conv_b"][l].reshape(248, 128)
        vecs[vb + V_BB:vb + V_BB + 8] = inp["b_conv_b"][l].reshape(8, 128)
        vecs[vb + V_LG:vb + V_LG + 8] = inp["ln_b_g"][l].reshape(8, 128)
        vecs[vb + V_LB:vb + V_LB + 8] = inp["ln_b_b"][l].reshape(8, 128)
        vecs[vb + V_PS:vb + V_PS + 8] = inp["pool_scale"][l].reshape(8, 128)
        vecs[vb + V_BA:vb + V_BA + 48] = inp["b_ada"][l].reshape(48, 128)
    vecs[V_FG:V_FG + 8] = inp["final_g"].reshape(8, 128)
    return wall, wada, vecs


def prep_core(inp, c):
    cache = np.concatenate([inp["cache_conv_a"], inp["cache_conv_b"], inp["cache_pool"]], axis=2)
    return {
        "xp": np.ascontiguousarray(inp["x_prompt"][c]),
        "xs": np.ascontiguousarray(inp["x_sample"][4 * c:4 * c + 4]),
        "cvec": np.ascontiguousarray(np.concatenate([inp["c_prompt"][c:c + 1], inp["c_sample"][4 * c:4 * c + 4]], axis=0)),
        "cache": np.ascontiguousarray(cache[:, 4 * c:4 * c + 4]),
    }


_CACHE = {}


def kernel(**inputs):
    inp = {k: np.asarray(v, dtype=np.float32) for k, v in inputs.items()}
    wall, wada, vecs = prep_shared(inp)
    if "nc" not in _CACHE:
        _CACHE["nc"] = build_program()[0]
    nc = _CACHE["nc"]
    in_maps = []
    for c in range(NCORES):
        m = prep_core(inp, c)
        m.update({"wall": wall, "wada": wada, "vecs": vecs})
        in_maps.append(m)
    res = run_bass_kernel_spmd(nc, in_maps, core_ids=list(range(NCORES)))
    R = res.results
    y_prompt = np.stack([R[c]["yp"] for c in range(NCORES)], axis=0).astype(np.float32)
    y_sample = np.concatenate([R[c]["ys"] for c in range(NCORES)], axis=0).astype(np.float32)
    stp = np.stack([R[c]["stp"] for c in range(NCORES)], axis=1)
    sts = np.concatenate([R[c]["sts"] for c in range(NCORES)], axis=1)
    return (y_prompt, y_sample,
            np.ascontiguousarray(stp[:, :, 0:2]), np.ascontiguousarray(stp[:, :, 2:32]), np.ascontiguousarray(stp[:, :, 32:47]),
            np.ascontiguousarray(sts[:, :, 0:2]), np.ascontiguousarray(sts[:, :, 2:32]), np.ascontiguousarray(sts[:, :, 32:47]))
```

```python
import numpy as np
import concourse.bass as bass
import concourse.mybir as mybir
from concourse.bass_utils import run_bass_kernel_spmd

F32 = mybir.dt.float32
BF16 = mybir.dt.bfloat16
AF = mybir.ActivationFunctionType
ALU = mybir.AluOpType

D = 1024
KC = 8
TP = 1024
TS = 64
T = TP + TS
NPASS = 4
NL = 2
DFF = 2816
NFB = DFF // 128
NIN = 9216
EPS = 1e-6
NCORES = 8
HR = 47
SUBT = [(0, 512, 0), (512, 512, 0), (1024, 64, 1)]
POOLW = (2, 4, 8, 16)
TD = 14

U_CB = 0
U_A2 = 48
U_B2 = 64
U_C2 = 80
U_O = 96
U_FI = 104
U_FO = 148
U_CV = 172
U_PER_LAYER = 204

V_N1 = 0
V_N2 = 8
V_CA = 16
V_CB = 40
V_BB = 288
V_LG = 296
V_LB = 304
V_PS = 312
V_BA = 320
V_PER_LAYER = 368
V_FG = 2 * V_PER_LAYER
V_ROWS = 768


class View:
    __slots__ = ("ap", "root", "ivs")

    def __init__(self, ap, root, ivs):
        self.ap = ap
        self.root = root
        self.ivs = ivs


class Buf:
    def __init__(self, space, root, tensor_ap, byte_off, dtype, K, C, nparts=128):
        self.space = space
        self.root = root
        self.K = K
        self.C = C
        self.es = 4 if dtype == F32 else 2
        self.byte_off = byte_off
        self.rs = C * self.es
        nb = K * C * self.es
        assert byte_off % 4 == 0 and nb % 4 == 0
        flat = tensor_ap[:, byte_off // 4:(byte_off + nb) // 4]
        if dtype != F32:
            flat = flat.bitcast(dtype)
        self.ap2 = flat
        self.ap3 = flat.rearrange("p (k c) -> p k c", k=K) if K > 1 else None
        self.nbytes = nb

    def v(self, k0=None, k1=None, c0=0, c1=None, p0=0, p1=128):
        if c1 is None:
            c1 = self.C
        if self.K == 1:
            ap = self.ap2[p0:p1, c0:c1]
            ivs = [(self.byte_off + c0 * self.es, self.byte_off + c1 * self.es)]
            return View(ap, (self.space, self.root), ivs)
        if k0 is None:
            k0, k1 = 0, self.K
        if k1 is None:
            ap = self.ap3[p0:p1, k0, c0:c1]
            ks = [k0]
        else:
            ap = self.ap3[p0:p1, k0:k1, c0:c1]
            ks = list(range(k0, k1))
        if c0 == 0 and c1 == self.C:
            ivs = [(self.byte_off + ks[0] * self.rs, self.byte_off + (ks[-1] + 1) * self.rs)]
        else:
            ivs = [(self.byte_off + k * self.rs + c0 * self.es, self.byte_off + k * self.rs + c1 * self.es) for k in ks]
        return View(ap, (self.space, self.root), ivs)


def _overlap(a, b):
    for (l0, h0) in a:
        for (l1, h1) in b:
            if l0 < h1 and l1 < h0:
                return True
    return False


def _covered(inner, outer):
    for (l0, h0) in inner:
        ok = False
        for (l1, h1) in outer:
            if l1 <= l0 and h0 <= h1:
                ok = True
                break
        if not ok:
            return False
    return True


class Op:
    __slots__ = ("eng", "emit", "deps", "sig", "sem", "ticket", "is_dma", "dkey", "ndma", "idx")


class Prog:
    COMPUTE = ("pe", "act", "dve", "pool")

    def __init__(self, same_engine_sync=True):
        self.ops = []
        self.acc = {}
        self.same_engine_sync = same_engine_sync

    def add(self, eng, emit, reads=(), writes=(), dkey=None, ndma=0, extra_deps=()):
        op = Op()
        op.eng = eng
        op.emit = emit
        op.is_dma = dkey is not None
        op.dkey = dkey
        op.ndma = ndma
        op.sig = op.is_dma
        op.idx = len(self.ops)
        deps = set(extra_deps)
        for r in reads:
            lst = self.acc.setdefault(r.root, [])
            for e in lst:
                if e[3] and _overlap(e[2], r.ivs):
                    deps.add(e[0])
            if not op.is_dma:
                lst[:] = [e for e in lst if e[0] == op.idx or not ((not e[3]) and e[1] == eng and e[2] == r.ivs and not self.ops[e[0]].is_dma)]
            lst.append([op.idx, eng, r.ivs, False])
        for w in writes:
            lst = self.acc.setdefault(w.root, [])
            keep = []
            for e in lst:
                if e[0] == op.idx:
                    keep.append(e)
                    continue
                if _overlap(e[2], w.ivs):
                    deps.add(e[0])
                    if _covered(e[2], w.ivs):
                        continue
                keep.append(e)
            keep.append([op.idx, eng, w.ivs, True])
            lst[:] = keep
        deps.discard(op.idx)
        op.deps = deps
        self.ops.append(op)
        return op.idx

    def finalize_and_emit(self, nc, block_engines, sems, dsems):
        ops = self.ops
        for op in ops:
            for d in op.deps:
                x = ops[d]
                if x.is_dma:
                    continue
                if x.eng == op.eng and not op.is_dma:
                    if op.eng == "pe" or not self.same_engine_sync:
                        continue
                x.sig = True
        cnt = {e: 0 for e in self.COMPUTE}
        dcnt = {}
        for op in ops:
            if op.is_dma:
                dcnt[op.dkey] = dcnt.get(op.dkey, 0) + 16 * op.ndma
                op.sem = dsems[op.dkey]
                op.ticket = dcnt[op.dkey]
            elif op.sig:
                cnt[op.eng] += 1
                op.sem = sems[op.eng]
                op.ticket = cnt[op.eng]
        streams = {}
        for op in ops:
            streams.setdefault(op.eng, []).append(op)
        self.stats = {e: len(s) for e, s in streams.items()}
        self.sigcnt = cnt

        def make_stream(eng_name):
            def fn(e):
                waited = {}
                for op in streams.get(eng_name, []):
                    need = {}
                    for d in op.deps:
                        x = ops[d]
                        if not x.is_dma and x.eng == op.eng and not op.is_dma:
                            if op.eng == "pe" or not self.same_engine_sync:
                                continue
                        key = id(x.sem)
                        if key not in need or need[key][1] < x.ticket:
                            need[key] = (x.sem, x.ticket)
                    for key, (sem, val) in need.items():
                        if waited.get(key, 0) < val:
                            e.wait_ge(sem, val)
                            waited[key] = val
                    if op.emit is not None:
                        if op.is_dma:
                            op.emit(e, op.sem)
                        else:
                            ins = op.emit(e)
                            if op.sig:
                                ins.then_inc(op.sem, 1)
            return fn

        return make_stream


def build_program(npass=NPASS, nlayer=NL, same_engine_sync=True, wring_units=11):
    nc = bass.Bass("TRN2", target_bir_lowering=False)
    ntok_p = npass * TP
    xp = nc.dram_tensor("xp", [NPASS * TP, D], F32, kind="ExternalInput").ap()
    xs = nc.dram_tensor("xs", [NPASS, TS, D], F32, kind="ExternalInput").ap()
    cvec = nc.dram_tensor("cvec", [5, D], F32, kind="ExternalInput").ap()
    cache = nc.dram_tensor("cache", [NL, NPASS, HR, D], F32, kind="ExternalInput").ap()
    vecs = nc.dram_tensor("vecs", [V_ROWS, 128], F32, kind="ExternalInput").ap()
    wall = nc.dram_tensor("wall", [NL * U_PER_LAYER, 128, 1024], F32, kind="ExternalInput").ap()
    wada = nc.dram_tensor("wada", [NL * 48, 128, 1024], F32, kind="ExternalInput").ap()
    yp = nc.dram_tensor("yp", [NPASS * TP, D], F32, kind="ExternalOutput").ap()
    ys = nc.dram_tensor("ys", [NPASS, TS, D], F32, kind="ExternalOutput").ap()
    stp = nc.dram_tensor("stp", [NL, HR, D], F32, kind="ExternalOutput").ap()
    sts = nc.dram_tensor("sts", [NL, NPASS, HR, D], F32, kind="ExternalOutput").ap()

    P = Prog(same_engine_sync=same_engine_sync)

    off = [0]

    def alloc(nbytes):
        o = off[0]
        off[0] += (nbytes + 3) // 4 * 4
        return o

    o_X = alloc(KC * T * 4)
    o_H = alloc(KC * T * 2)
    o_PA = alloc(KC * T * 2)
    o_PC = alloc(KC * T * 2)
    o_BIG = alloc(KC * T * 4)
    o_PB = alloc(KC * T * 2)
    o_WR = alloc(wring_units * 2048)
    CU = 2 + TP + 2 + TS
    CV = 30 + TP + 30 + TS
    CP = 15 + TP + 15 + TS
    CT = 1152
    o_UF = alloc(CU * 4)
    o_VF = alloc(2 * CV * 2)
    o_PF = alloc(CP * 4)
    o_T1 = alloc(CT * 4)
    o_T2 = alloc(CT * 4)
    o_T3 = alloc(CT * 4)
    o_ST = alloc(3 * 512 * 4)
    o_VEC = alloc(V_ROWS * 4)
    o_ID = alloc(128 * 4)
    o_ONE = alloc(128 * 2)
    o_HP = alloc(NL * KC * HR * 4)
    o_HS = alloc(NL * KC * HR * 4)
    o_ADA = alloc(NL * 48 * 5 * 4)
    o_AA = alloc(NL * 2 * KC * 5 * 4)
    o_SC = alloc(KC * 5 * 2 + 16)
    o_IC = alloc(4 * 16 * 4)
    o_EPS = alloc(4)
    o_BGS = alloc(T * 4)
    arena_bytes = off[0]
    assert arena_bytes <= (nc.sbuf_top - nc.sbuf_base - 64), arena_bytes

    import contextlib
    es = contextlib.ExitStack()
    with es:
        arena_t = es.enter_context(nc.sbuf_tensor("arena", [128, arena_bytes // 4], F32))
        psum_t = es.enter_context(nc.psum_tensor("psum", [128, 8 * 512], F32))
        A = arena_t[:, :]
        PSAP = psum_t[:, :]

        def sb(root, o, dtype, K, C):
            return Buf("sb", root, A, o, dtype, K, C)

        X = sb("X", o_X, F32, KC, T)
        H = sb("H", o_H, BF16, KC, T)
        PA = sb("PA", o_PA, BF16, KC, T)
        PC = sb("PC", o_PC, BF16, KC, T)
        BIG = sb("BIGPB", o_BIG, F32, KC, T)
        PB = sb("BIGPB", o_PB, BF16, KC, T)
        ACTB = sb("BIGPB", o_BIG, BF16, NFB, T)
        WR = [sb("WR%d" % i, o_WR + i * 2048, BF16, KC, 128) for i in range(wring_units)]
        UF = sb("UF", o_UF, F32, 1, CU)
        VFB = [sb("VF%d" % i, o_VF + i * CV * 2, BF16, 1, CV) for i in range(2)]
        PF = sb("PF", o_PF, F32, 1, CP)
        SQ2 = sb("UVP", o_UF, BF16, KC, 512)
        assert o_VF == o_UF + CU * 4 and o_PF == o_VF + 2 * CV * 2 and CU * 4 + 2 * CV * 2 >= KC * 512 * 2
        OSTG = [sb("UVP", o_UF, F32, 1, 1024), sb("UVP", o_PF, F32, 1, 1024)]
        SQH = sb("H", o_H, BF16, KC, 512)
        for b_ in (UF, PF, VFB[0], VFB[1]):
            b_.root = "UVP"
        T1 = sb("T1", o_T1, F32, 1, CT)
        T2 = sb("T2", o_T2, F32, 1, CT)
        T3 = sb("T3", o_T3, F32, 1, CT)
        SQB = sb("T1", o_T1, BF16, KC, 512)
        STG = [sb("T1", o_T1, F32, 1, 1024), sb("T2", o_T2 + 0, F32, 1, 1024)]
        STAT = sb("STAT", o_ST, F32, 3, 512)
        VEC = sb("VEC", o_VEC, F32, 1, V_ROWS)
        IDENT = sb("ID", o_ID, F32, 1, 128)
        ONES = sb("ONE", o_ONE, BF16, 1, 128)
        HP = sb("HP", o_HP, F32, NL * KC, HR)
        HS = sb("HS", o_HS, F32, NL * KC, HR)
        ADA = sb("ADA", o_ADA, F32, NL * 48, 5)
        AA = sb("AA", o_AA, F32, NL * 2 * KC, 5)
        SC = sb("SC", o_SC, BF16, KC, 5)
        IC = sb("IC", o_IC, F32, 4, 16)
        EPSC = sb("EPSC", o_EPS, F32, 1, 1)
        BGS = sb("BGS", o_BGS, F32, 1, T)
        for b in (T1, T2, SQB, STG[0], STG[1]):
            b.root = "T12"
        assert CT * 4 >= 4096 and o_T2 == o_T1 + CT * 4
        assert 2 * CT * 4 >= KC * 512 * 2

        PSB = [Buf("ps", "PS%d" % i, PSAP, i * 2048, F32, 1, 512) for i in range(8)]
        psi = [0]

        def bank():
            b = PSB[psi[0] % 8]
            psi[0] += 1
            return b

        def mm(out, lhsT, rhs, start, stop):
            P.add("pe", lambda e, o=out.ap, l=lhsT.ap, r=rhs.ap, s=start, t=stop: e.matmul(o, l, r, start=s, stop=t),
                  reads=[lhsT, rhs], writes=[out])

        def tr(out, in_, ident):
            P.add("pe", lambda e, o=out.ap, i=in_.ap, d=ident.ap: e.transpose(o, i, d), reads=[in_, ident], writes=[out])

        def act(out, in_, func, scale=1.0, bias=0.0, eng="act"):
            rd = [in_]
            sc = scale
            bi = bias
            if isinstance(scale, View):
                rd.append(scale)
                sc = scale.ap
            if isinstance(bias, View):
                rd.append(bias)
                bi = bias.ap
            P.add("act", lambda e, o=out.ap, i=in_.ap, f=func, s=sc, b=bi: e.activation(o, i, f, bias=b, scale=s),
                  reads=rd, writes=[out])

        def tt(out, a, b, op, eng="dve"):
            P.add(eng, lambda e, o=out.ap, x=a.ap, y=b.ap, p=op: e.tensor_tensor(o, x, y, p), reads=[a, b], writes=[out])

        def ts(out, a, s1, s2, op0, op1=None, eng="dve"):
            rd = [a]
            v1, v2 = s1, s2
            if isinstance(s1, View):
                rd.append(s1)
                v1 = s1.ap
            if isinstance(s2, View):
                rd.append(s2)
                v2 = s2.ap
            if op1 is None:
                P.add(eng, lambda e, o=out.ap, x=a.ap, u=v1, p=op0: e.tensor_scalar(o, x, u, None, p), reads=rd, writes=[out])
            else:
                P.add(eng, lambda e, o=out.ap, x=a.ap, u=v1, w=v2, p=op0, q=op1: e.tensor_scalar(o, x, u, w, p, q),
                      reads=rd, writes=[out])

        def stt(out, in0, scalar, in1, op0, op1, eng="dve"):
            rd = [in0, in1]
            sv = scalar
            if isinstance(scalar, View):
                rd.append(scalar)
                sv = scalar.ap
            P.add(eng, lambda e, o=out.ap, x=in0.ap, s=sv, y=in1.ap, p=op0, q=op1: e.scalar_tensor_tensor(o, x, s, y, p, q),
                  reads=rd, writes=[out])

        def powm05(v):
            P.add("pool", lambda e, a=v.ap: e.tensor_single_scalar(a, a, -0.5, ALU.pow), reads=[v], writes=[v])

        def cp(out, in_, eng="dve"):
            P.add(eng, lambda e, o=out.ap, i=in_.ap: e.tensor_copy(o, i), reads=[in_], writes=[out])

        def dma(queue, dkey, pairs, reads=(), writes=()):
            def emit(e, sem, pairs=pairs):
                for (o, i) in pairs:
                    e.dma_start(out=o, in_=i).then_inc(sem, 16)
            return P.add(queue, emit, reads=reads, writes=writes, dkey=dkey, ndma=len(pairs))

        def vcol(row):
            return VEC.v(c0=row, c1=row + 1)

        wr_next = [0]

        def load_unit(src_ap_full, nk=8):
            slot = WR[wr_next[0] % wring_units]
            key = "WR%d" % (wr_next[0] % wring_units)
            wr_next[0] += 1
            dst = slot.v(0, nk)
            dma("pool", key, [(dst.ap, src_ap_full[:, 0:nk * 128].rearrange("p (k n) -> p k n", k=nk))], writes=[dst])
            return slot

        pending = []
        loaded = []

        class WStream:
            def __init__(self):
                self.sched = []
                self.issued = 0
                self.slots = {}

            def plan(self, src, nk=8):
                self.sched.append((src, nk))
                return len(self.sched) - 1

        DEFER_ADA1 = nlayer > 1

        def sched_for(p):
            lst = []
            for l in range(nlayer):
                base = l * U_PER_LAYER
                for cb in range(KC + 1):
                    if cb < KC:
                        for j in (0, 1, 2, 5):
                            lst.append(("wall", base + U_CB + cb * 6 + j, 8))
                    if cb >= 1:
                        for u in range(TD // 8, 4):
                            lst.append(("wall", base + U_CV + (cb - 1) * 4 + u, 8 if u < 3 else 7))
                    if cb < KC:
                        lst.append(("wall", base + U_CB + cb * 6 + 3, 8))
                        lst.append(("wall", base + U_CB + cb * 6 + 4, 8))
                        if DEFER_ADA1 and p == 0 and l == 0:
                            for i in range(6):
                                lst.append(("wada", 48 + cb * 6 + i, 8))
                for u in range(U_A2, U_CV):
                    nk = 8
                    if U_C2 <= u < U_O and (u - U_C2) % 2 == 0:
                        nk = 2
                    if U_FO <= u and (u - U_FO) % 3 == 2:
                        nk = 6
                    lst.append(("wall", base + u, nk))
            return lst

        wsched = []
        for p_ in range(npass):
            wsched += sched_for(p_)
        wpos = {"issued": 0, "used": 0, "slots": {}}
        total_units = [0]

        def w_prefetch(upto):
            while wpos["issued"] < upto and wpos["issued"] < total_units[0]:
                g = wpos["issued"]
                (tn, ui, nk) = wsched[g]
                wpos["slots"][g] = load_unit(wall[ui] if tn == "wall" else wada[ui], nk)
                wpos["issued"] += 1

        def w_take(n):
            g = wpos["used"]
            w_prefetch(g + wring_units)
            wpos["used"] += n
            return [wpos["slots"].pop(g + i) for i in range(n)]

        def w_next():
            return w_take(1)[0]

        total_units[0] = len(wsched)

        P.add("pool", lambda e: e.memset(IDENT.v().ap, 0.0), writes=[IDENT.v()])
        P.add("pool", lambda e: e.iota(IDENT.v().ap, [[1, 128]], channel_multiplier=-1, allow_small_or_imprecise_dtypes=True),
              writes=[IDENT.v()], reads=[IDENT.v()])
        P.add("pool", lambda e: e.tensor_single_scalar(IDENT.v().ap, IDENT.v().ap, 0.0, ALU.is_equal),
              reads=[IDENT.v()], writes=[IDENT.v()])
        P.add("pool", lambda e: e.memset(ONES.v().ap, 1.0), writes=[ONES.v()])
        P.add("pool", lambda e: e.memset(EPSC.v().ap, EPS), writes=[EPSC.v()])
        P.add("pool", lambda e: e.memset(HP.v().ap, 0.0), writes=[HP.v()])
        for g, w in enumerate(POOLW):
            for j in range(15):
                val = 1.0 / min(j + 1, w)
                P.add("pool", lambda e, a=IC.v(g, None, j, j + 1).ap, v=val: e.memset(a, v), writes=[IC.v(g, None, j, j + 1)])

        for i in range(V_ROWS // 128):
            s = STG[i % 2]
            dma("sp", "STG%d" % (i % 2), [(s.v(c0=0, c1=128).ap, vecs[i * 128:(i + 1) * 128, :])], writes=[s.v(c0=0, c1=128)])
            b = bank()
            tr(b.v(c0=0, c1=128), s.v(c0=0, c1=128), IDENT.v())
            cp(VEC.v(c0=i * 128, c1=(i + 1) * 128), b.v(c0=0, c1=128), eng="dve")
        s = STG[0]
        dma("sp", "STG0", [(s.v(c0=0, c1=1024, p1=5).ap, cvec[:, :])], writes=[s.v()])
        b = bank()
        for k in range(KC):
            tr(b.v(c0=k * 5, c1=k * 5 + 5), s.v(c0=k * 128, c1=(k + 1) * 128, p1=5), IDENT.v(c0=0, c1=5, p1=5))
        act(T3.v(c0=0, c1=40), b.v(c0=0, c1=40), AF.Sigmoid)
        P.add("dve", lambda e, o=SC.ap2[:, 0:40], x=T3.v(c0=0, c1=40).ap, y=b.v(c0=0, c1=40).ap: e.tensor_tensor(o, x, y, ALU.mult),
              reads=[T3.v(c0=0, c1=40), b.v(c0=0, c1=40)], writes=[SC.v()])

        def ada_evac(l, b, j0, nj):
            adal = Buf("sb", "ADA", A, o_ADA + l * 48 * 5 * 4, F32, 48, 5)
            r0 = l * V_PER_LAYER + V_BA + j0
            bview = View(VEC.ap2[:, r0:r0 + nj].unsqueeze(2).to_broadcast([128, nj, 5]), VEC.v().root, VEC.v().ivs)
            pview = View(b.ap2[:, 0:nj * 5].rearrange("p (j s) -> p j s", j=nj), b.v().root, b.v(c0=0, c1=nj * 5).ivs)
            tt(adal.v(j0, j0 + nj), pview, bview, ALU.add)

        def ada_derive(l):
            adal = Buf("sb", "ADA", A, o_ADA + l * 48 * 5 * 4, F32, 48, 5)
            for which, (vrow, aoff) in enumerate(((V_N1, 8), (V_N2, 32))):
                aab = Buf("sb", "AA", A, o_AA + (l * 2 + which) * KC * 5 * 4, F32, KC, 5)
                gview = View(VEC.ap2[:, l * V_PER_LAYER + vrow:l * V_PER_LAYER + vrow + 8].unsqueeze(2).to_broadcast([128, 8, 5]),
                             VEC.v().root, VEC.v().ivs)
                stt(aab.v(), adal.v(aoff, aoff + 8), 1.0, gview, ALU.add, ALU.mult)

        for l in range(1 if DEFER_ADA1 else nlayer):
            b = bank()
            for j in range(48):
                slot = WR[wr_next[0] % wring_units]
                key = "WR%d" % (wr_next[0] % wring_units)
                wr_next[0] += 1
                dst = slot.v(0, 8)
                dma("pool", key, [(dst.ap, wada[l * 48 + j].rearrange("p (k n) -> p k n", k=8))], writes=[dst])
                for k in range(KC):
                    mm(b.v(c0=j * 5, c1=j * 5 + 5), slot.v(k, None), SC.v(k, None), k == 0, k == KC - 1)
            ada_evac(l, b, 0, 48)
            ada_derive(l)

        def ada1_step(cb):
            us = w_take(6)
            b = bank()
            for i, slot in enumerate(us):
                for k in range(KC):
                    mm(b.v(c0=i * 5, c1=i * 5 + 5), slot.v(k, None), SC.v(k, None), k == 0, k == KC - 1)
            ada_evac(1, b, cb * 6, 6)
            if cb == KC - 1:
                ada_derive(1)

        def ada_col(l, j, s):
            adal = Buf("sb", "ADA", A, o_ADA + l * 48 * 5 * 4, F32, 48, 5)
            return adal.v(j, None, s, s + 1)

        def aa_col(l, which, k, s):
            aab = Buf("sb", "AA", A, o_AA + (l * 2 + which) * KC * 5 * 4, F32, KC, 5)
            return aab.v(k, None, s, s + 1)

        def hist(HB, l, cb, c0, c1):
            return HB.v(l * KC + cb, None, c0, c1)

        def rms_all(sqb, presq=False):
            rs = []
            for si, (c0, n, sq_i) in enumerate(SUBT):
                b = bank()
                for k in range(KC):
                    if presq:
                        sq = sqb.v(k, None, c0, c0 + n)
                    else:
                        sq = sqb.v(k, None, 0, n)
                        xv = X.v(k, None, c0, c0 + n)
                        if k % 2 == 0:
                            act(sq, xv, AF.Square)
                        else:
                            tt(sq, xv, xv, ALU.mult)
                    mm(b.v(c0=0, c1=n), ONES.v(), sq, k == 0, k == KC - 1)
                r = STAT.v(si, None, 0, n)
                act(r, b.v(c0=0, c1=n), AF.Sqrt, scale=1.0 / D, bias=EPSC.v())
                P.add("dve", lambda e, a=r.ap: e.reciprocal(a, a), reads=[r], writes=[r])
                rs.append(r)
            return rs

        def ada_norm(l, which, seqslots, presq=False):
            shoff = 0 if which == 0 else 24
            rs = rms_all(PA, True) if presq else rms_all(SQB)
            for si, (c0, n, sq_i) in enumerate(SUBT):
                s = seqslots[sq_i]
                for k in range(KC):
                    tmp = T3.v(c0=(k % 2) * 512, c1=(k % 2) * 512 + n)
                    tt(tmp, X.v(k, None, c0, c0 + n), rs[si], ALU.mult)
                    act(H.v(k, None, c0, c0 + n), tmp, AF.Identity, scale=aa_col(l, which, k, s), bias=ada_col(l, shoff + k, s))

        def norm_st(l, which, seqslots, si, sqb, part="ABC"):
            shoff = 0 if which == 0 else 24
            (c0, n, sq_i) = SUBT[si]
            if "A" in part:
                for k in range(KC):
                    sq = sqb.v(k, None, 0, n)
                    xv = X.v(k, None, c0, c0 + n)
                    if k % 2 == 0:
                        act(sq, xv, AF.Square)
                    else:
                        tt(sq, xv, xv, ALU.mult)
            r = STAT.v(si, None, 0, n)
            if "B" in part:
                b = bank()
                for k in range(KC):
                    mm(b.v(c0=0, c1=n), ONES.v(), sqb.v(k, None, 0, n), k == 0, k == KC - 1)
                act(r, b.v(c0=0, c1=n), AF.Sqrt, scale=1.0 / D, bias=EPSC.v())
                P.add("dve", lambda e, a=r.ap: e.reciprocal(a, a), reads=[r], writes=[r])
            if "C" in part:
                s_ = seqslots[sq_i]
                for k in range(KC):
                    tmp = T3.v(c0=(k % 2) * 512, c1=(k % 2) * 512 + n)
                    tt(tmp, X.v(k, None, c0, c0 + n), r, ALU.mult)
                    act(H.v(k, None, c0, c0 + n), tmp, AF.Identity, scale=aa_col(l, which, k, s_), bias=ada_col(l, shoff + k, s_))

        def conv_b(l, cbp):
            vbq = l * V_PER_LAYER
            u0 = TD // 8
            dv = w_take(4 - u0)
            vfbp = VFB[cbp % 2]
            for (c0, n, sq_i) in SUBT:
                bk = bank()
                off = c0 if sq_i == 0 else TP + 30 + (c0 - TP)
                for j in range(TD, 31):
                    mm(bk.v(c0=0, c1=n), dv[j // 8 - u0].v(j % 8, None), vfbp.v(c0=off + j, c1=off + j + n), j == TD, j == 30)
                bv = BIG.v(cbp, None, c0, c0 + n)
                stt(bv, bk.v(c0=0, c1=n), vcol(vbq + V_BB + cbp), bv, ALU.add, ALU.add)

        def layer(p, l, seqslots, presq=False):
            vb = l * V_PER_LAYER
            first_prompt = (p == 0)
            deferred = []

            def drain(n=1):
                for _ in range(min(n, len(deferred))):
                    deferred.pop(0)()

            ada_norm(l, 0, seqslots, presq)
            for cb in range(KC):
                wbg, wcg, wha, wpin = w_take(4)
                bgb = []
                for (c0, n, sq_i) in SUBT:
                    bc, bh, bb = bank(), bank(), bank()
                    for k in range(KC):
                        mm(bc.v(c0=0, c1=n), wcg.v(k, None), H.v(k, None, c0, c0 + n), k == 0, k == KC - 1)
                    for k in range(KC):
                        mm(bh.v(c0=0, c1=n), wha.v(k, None), H.v(k, None, c0, c0 + n), k == 0, k == KC - 1)
                    for k in range(KC):
                        mm(bb.v(c0=0, c1=n), wbg.v(k, None), H.v(k, None, c0, c0 + n), k == 0, k == KC - 1)
                    act(T3.v(c0=c0, c1=c0 + n), bc.v(c0=0, c1=n), AF.Copy)
                    act(BGS.v(c0=c0, c1=c0 + n), bb.v(c0=0, c1=n), AF.Copy)
                    uoff = 2 + c0 if sq_i == 0 else 2 + TP + 2 + (c0 - TP)
                    tt(UF.v(c0=uoff, c1=uoff + n), bh.v(c0=0, c1=n), T3.v(c0=c0, c1=c0 + n), ALU.mult)
                    drain(1)
                cp(UF.v(c0=0, c1=2), hist(HP, l, cb, 0, 2), eng="pool")
                cp(UF.v(c0=2 + TP, c1=4 + TP), hist(HS, l, cb, 0, 2), eng="pool")
                NO = TP + 2 + TS
                wa = [vcol(vb + V_CA + j * 8 + cb) for j in range(3)]
                ts(T2.v(c0=0, c1=NO), UF.v(c0=0, c1=NO), wa[0], None, ALU.mult)
                stt(T2.v(c0=0, c1=NO), UF.v(c0=1, c1=1 + NO), wa[1], T2.v(c0=0, c1=NO), ALU.mult, ALU.add)
                stt(T2.v(c0=0, c1=NO), UF.v(c0=2, c1=2 + NO), wa[2], T2.v(c0=0, c1=NO), ALU.mult, ALU.add)
                drain(1)
                for i, (c0, n, sq_i) in enumerate(SUBT):
                    toff = c0 if sq_i == 0 else TP + 2 + (c0 - TP)
                    tt(PA.v(cb, None, c0, c0 + n), T2.v(c0=toff, c1=toff + n), BGS.v(c0=c0, c1=c0 + n), ALU.mult)
                    drain(1)
                cp(hist(HP, l, cb, 0, 2), UF.v(c0=TP, c1=TP + 2), eng="pool")
                cp(hist(HS, l, cb, 0, 2), UF.v(c0=CU - 2, c1=CU), eng="pool")
                for (c0, n, sq_i) in SUBT:
                    bp = bank()
                    for k in range(KC):
                        mm(bp.v(c0=0, c1=n), wpin.v(k, None), H.v(k, None, c0, c0 + n), k == 0, k == KC - 1)
                    poff = 15 + c0 if sq_i == 0 else 15 + TP + 15 + (c0 - TP)
                    act(PF.v(c0=poff, c1=poff + n), bp.v(c0=0, c1=n), AF.Copy)
                cp(PF.v(c0=0, c1=15), hist(HP, l, cb, 32, 47), eng="pool")
                cp(PF.v(c0=15 + TP, c1=30 + TP), hist(HS, l, cb, 32, 47), eng="pool")
                g = cb // 2
                w = POOLW[g]
                src = PF
                sh = 1
                tbufs = [T2, T3]
                ti = 0
                while sh < w:
                    dst = tbufs[ti % 2]
                    ti += 1
                    tt(dst.v(c0=sh, c1=CP), src.v(c0=sh, c1=CP), src.v(c0=0, c1=CP - sh), ALU.add)
                    src = dst
                    sh *= 2
                for (c0, n, sq_i) in SUBT:
                    poff = 15 + c0 if sq_i == 0 else 15 + TP + 15 + (c0 - TP)
                    stt(PC.v(cb, None, c0, c0 + n), src.v(c0=poff, c1=poff + n), 1.0 / w, PF.v(c0=poff, c1=poff + n),
                        ALU.mult, ALU.subtract)
                    drain(1)
                if first_prompt:
                    tmpv = UF.v(c0=0, c1=15)
                    tt(tmpv, src.v(c0=15, c1=30), IC.v(g, None, 0, 15), ALU.mult)
                    tt(PC.v(cb, None, 0, 15), tmpv, PF.v(c0=15, c1=30), ALU.subtract)
                cp(hist(HP, l, cb, 32, 47), PF.v(c0=TP, c1=TP + 15), eng="pool")
                cp(hist(HS, l, cb, 32, 47), PF.v(c0=CP - 15, c1=CP), eng="pool")
                vfb = VFB[cb % 2]
                drain(100)
                if cb >= 1:
                    conv_b(l, cb - 1)
                wga, wgb = w_take(2)
                cp(T1.v(c0=0, c1=30), hist(HP, l, cb, 2, 32), eng="pool")
                cp(T1.v(c0=30 + TP, c1=60 + TP), hist(HS, l, cb, 2, 32), eng="pool")
                for (c0, n, sq_i) in SUBT:
                    bgt, bga = bank(), bank()
                    for k in range(KC):
                        mm(bgt.v(c0=0, c1=n), wgb.v(k, None), H.v(k, None, c0, c0 + n), k == 0, k == KC - 1)
                    for k in range(KC):
                        mm(bga.v(c0=0, c1=n), wga.v(k, None), H.v(k, None, c0, c0 + n), k == 0, k == KC - 1)
                    voff = 30 + c0 if sq_i == 0 else 30 + TP + 30 + (c0 - TP)
                    vv = T1.v(c0=voff, c1=voff + n)
                    act(vv, bgt.v(c0=0, c1=n), AF.Sigmoid)
                    tt(vv, bga.v(c0=0, c1=n), vv, ALU.mult)
                act(vfb.v(c0=0, c1=CV), T1.v(c0=0, c1=CV), AF.Copy)
                cp(hist(HP, l, cb, 2, 32), T1.v(c0=TP, c1=TP + 30), eng="pool")
                cp(hist(HS, l, cb, 2, 32), T1.v(c0=CV - 30, c1=CV), eng="pool")
                wbv = [vcol(vb + V_CB + j * 8 + cb) for j in range(31)]
                bp_, bs_ = BIG.v(cb, None, 0, TP), BIG.v(cb, None, TP, T)
                so = TP + 30
                def tap0(bp_=bp_, bs_=bs_, w0=wbv[0]):
                    ts(bp_, T1.v(c0=0, c1=TP), w0, None, ALU.mult)
                    ts(bs_, T1.v(c0=so, c1=so + TS), w0, None, ALU.mult)
                deferred.append(tap0)
                for j in range(1, TD):
                    def tapj(j=j, bp_=bp_, bs_=bs_, wj=wbv[j]):
                        stt(bp_, T1.v(c0=j, c1=j + TP), wj, bp_, ALU.mult, ALU.add)
                        stt(bs_, T1.v(c0=so + j, c1=so + j + TS), wj, bs_, ALU.mult, ALU.add)
                    deferred.append(tapj)
                drain(2)
                if DEFER_ADA1 and p == 0 and l == 0:
                    ada1_step(cb)
            drain(100)
            conv_b(l, KC - 1)
            lnst = []
            def srow(si, n):
                if si == 0:
                    return STAT.v(0, None, 0, n), STAT.v(1, None, 0, n)
                if si == 1:
                    return STAT.v(2, None, 0, n), T3.v(c0=0, c1=n)
                return T3.v(c0=512, c1=512 + n), T3.v(c0=576, c1=576 + n)
            for si, (c0, n, sq_i) in enumerate(SUBT):
                b1, b2 = bank(), bank()
                for k in range(KC):
                    cbf = SQB.v(k, None, 0, n)
                    cp(cbf, BIG.v(k, None, c0, c0 + n))
                    mm(b1.v(c0=0, c1=n), ONES.v(), cbf, k == 0, k == KC - 1)
                for k in range(KC):
                    sq = SQ2.v(k, None, 0, n)
                    act(sq, BIG.v(k, None, c0, c0 + n), AF.Square)
                    mm(b2.v(c0=0, c1=n), ONES.v(), sq, k == 0, k == KC - 1)
                mean, rstd = srow(si, n)
                ts(mean, b1.v(c0=0, c1=n), 1.0 / D, None, ALU.mult)
                stt(rstd, mean, -1.0, mean, ALU.mult, ALU.mult)
                stt(rstd, b2.v(c0=0, c1=n), 1.0 / D, rstd, ALU.mult, ALU.add)
                act(rstd, rstd, AF.Sqrt, scale=1.0, bias=EPSC.v())
                P.add("dve", lambda e, a=rstd.ap: e.reciprocal(a, a), reads=[rstd], writes=[rstd])
            def ln_norm(k):
                for si, (c0, n, sq_i) in enumerate(SUBT):
                    mean, rstd = srow(si, n)
                    bv = BIG.v(k, None, c0, c0 + n)
                    tt(bv, bv, mean, ALU.subtract)
                    tt(bv, bv, rstd, ALU.mult)
                    act(PB.v(k, None, c0, c0 + n), bv, AF.Silu, scale=vcol(vb + V_LG + k), bias=vcol(vb + V_LB + k))
            for br in range(3):
                for ob in range(KC):
                    wy, wg = w_take(2)
                    if br == 0:
                        ln_norm(ob)
                    for (c0, n, sq_i) in SUBT:
                        by, bg_ = bank(), bank()
                        if br == 0:
                            for k in range(KC):
                                mm(by.v(c0=0, c1=n), wy.v(k, None), PA.v(k, None, c0, c0 + n), k == 0, k == KC - 1)
                        elif br == 1:
                            for k in range(KC):
                                mm(by.v(c0=0, c1=n), wy.v(k, None), PB.v(k, None, c0, c0 + n), k == 0, k == KC - 1)
                        else:
                            gq = ob // 2
                            for j in range(2):
                                mm(by.v(c0=0, c1=n), wy.v(j, None), PC.v(2 * gq + j, None, c0, c0 + n), j == 0, j == 1)
                        for k in range(KC):
                            mm(bg_.v(c0=0, c1=n), wg.v(k, None), H.v(k, None, c0, c0 + n), k == 0, k == KC - 1)
                        sg = T1.v(c0=c0, c1=c0 + n)
                        act(sg, bg_.v(c0=0, c1=n), AF.Sigmoid)
                        mv = BIG.v(ob, None, c0, c0 + n)
                        if br == 0:
                            tt(mv, by.v(c0=0, c1=n), sg, ALU.mult)
                        elif br == 1:
                            tmp = T2.v(c0=c0, c1=c0 + n)
                            tt(tmp, by.v(c0=0, c1=n), sg, ALU.mult)
                            tt(mv, mv, tmp, ALU.add)
                        else:
                            tmp = T2.v(c0=c0, c1=c0 + n)
                            stt(tmp, by.v(c0=0, c1=n), vcol(vb + V_PS + ob), sg, ALU.mult, ALU.mult)
                            tt(PA.v(ob, None, c0, c0 + n), mv, tmp, ALU.add)
            wos = w_take(8)
            def o_st(si):
                (c0, n, sq_i) = SUBT[si]
                for ob in range(KC):
                    bo = bank()
                    for k in range(KC):
                        mm(bo.v(c0=0, c1=n), wos[ob].v(k, None), PA.v(k, None, c0, c0 + n), k == 0, k == KC - 1)
                    xv = X.v(ob, None, c0, c0 + n)
                    stt(xv, bo.v(c0=0, c1=n), ada_col(l, 16 + ob, seqslots[sq_i]), xv, ALU.mult, ALU.add)

            SQ3 = sb("PC", o_PC, BF16, KC, 512)
            o_st(0)
            norm_st(l, 1, seqslots, 0, SQ2, "A")
            o_st(2)
            norm_st(l, 1, seqslots, 0, SQ2, "BC")
            norm_st(l, 1, seqslots, 2, SQ3, "A")
            o_st(1)
            norm_st(l, 1, seqslots, 2, SQ3, "BC")
            norm_st(l, 1, seqslots, 1, SQ2, "ABC")

            def ffn_in(fb, si, wgt, wup):
                (c0, n, sq_i) = SUBT[si]
                bgt, bup = bank(), bank()
                for k in range(KC):
                    mm(bgt.v(c0=0, c1=n), wgt.v(k, None), H.v(k, None, c0, c0 + n), k == 0, k == KC - 1)
                for k in range(KC):
                    mm(bup.v(c0=0, c1=n), wup.v(k, None), H.v(k, None, c0, c0 + n), k == 0, k == KC - 1)
                gs = T1.v(c0=c0, c1=c0 + n)
                act(gs, bgt.v(c0=0, c1=n), AF.Silu)
                tt(ACTB.v(fb, None, c0, c0 + n), bup.v(c0=0, c1=n), gs, ALU.mult)

            NG = 3
            w6 = w_take(2 * NG)
            for grp in ((0, 2), (1,)):
                for j in range(NG):
                    for si in grp:
                        ffn_in(j, si, w6[2 * j], w6[2 * j + 1])
            for fb in range(NG, NFB):
                wgt, wup = w_take(2)
                for si in range(3):
                    ffn_in(fb, si, wgt, wup)
            for ob in range(KC):
                wf = w_take(3)
                for (c0, n, sq_i) in SUBT:
                    bo = bank()
                    for f in range(NFB):
                        mm(bo.v(c0=0, c1=n), wf[f // 8].v(f % 8, None), ACTB.v(f, None, c0, c0 + n), f == 0, f == NFB - 1)
                    xv = X.v(ob, None, c0, c0 + n)
                    stt(xv, bo.v(c0=0, c1=n), ada_col(l, 40 + ob, seqslots[sq_i]), xv, ALU.mult, ALU.add)
                    if (ob + (0 if sq_i == 0 else 1)) % 2 == 0:
                        act(PA.v(ob, None, c0, c0 + n), xv, AF.Square)
                    else:
                        tt(PA.v(ob, None, c0, c0 + n), xv, xv, ALU.mult)

        store_ops = []
        ostg_i = [0]

        def next_ostg():
            i = ostg_i[0] % 2
            ostg_i[0] += 1
            return OSTG[i], "OSTG%d" % i

        def pass_tiles(p):
            tl = [(xp[p * TP + i * 128:p * TP + (i + 1) * 128, :], 128, i * 128) for i in range(TP // 128)]
            tl.append((xs[p], TS, TP))
            return tl

        def load_dma(p, i):
            (src, nt, c0) = pass_tiles(p)[i]
            s_ = STG[i % 2]
            dma("sp", "STG%d" % (i % 2), [(s_.v(c0=0, c1=1024, p1=nt).ap, src)], writes=[s_.v()])

        def load_xpose(p, i):
            (src, nt, c0) = pass_tiles(p)[i]
            s_ = STG[i % 2]
            for half in range(2):
                b = bank()
                for kk in range(4):
                    k = half * 4 + kk
                    tr(b.v(c0=kk * 128, c1=kk * 128 + nt), s_.v(c0=k * 128, c1=(k + 1) * 128, p1=nt),
                       IDENT.v(c0=0, c1=nt, p1=nt))
                src_v = View(b.ap2[:, 0:512].rearrange("p (k n) -> p k n", k=4)[:, :, 0:nt], b.v().root, b.v().ivs)
                P.add("act", lambda e, o=X.v(half * 4, half * 4 + 4, c0, c0 + nt).ap, i_=src_v.ap: e.copy(o, i_),
                      reads=[src_v], writes=[X.v(half * 4, half * 4 + 4, c0, c0 + nt)])

        def load_caches(p):
            for l in range(nlayer):
                s_ = STG[l % 2]
                dma("sp", "STG%d" % (l % 2), [(s_.v(c0=0, c1=1024, p1=HR).ap, cache[l, p])], writes=[s_.v()])
                b = bank()
                for k in range(KC):
                    tr(b.v(c0=k * HR, c1=(k + 1) * HR), s_.v(c0=k * 128, c1=(k + 1) * 128, p1=HR), IDENT.v(c0=0, c1=HR, p1=HR))
                src_v = View(b.ap2[:, 0:KC * HR].rearrange("p (k n) -> p k n", k=KC), b.v().root, b.v(c0=0, c1=KC * HR).ivs)
                P.add("act", lambda e, o=HS.v(l * KC, (l + 1) * KC).ap, i_=src_v.ap: e.copy(o, i_),
                      reads=[src_v], writes=[HS.v(l * KC, (l + 1) * KC)])

        def out_tile(p, i, rs):
            (src, nt, c0) = pass_tiles(p)[i]
            si = 0 if c0 < 512 else (1 if c0 < TP else 2)
            t0 = c0 - SUBT[si][0]
            yt = Buf("sb", "T3", A, o_T3, F32, KC, 128)
            for k in range(KC):
                stt(yt.v(k, None, 0, nt), X.v(k, None, c0, c0 + nt), vcol(V_FG + k),
                    STAT.v(si, None, t0, t0 + nt), ALU.mult, ALU.mult)
            s_, key = next_ostg()
            for half in range(2):
                b = bank()
                for kk in range(4):
                    k = half * 4 + kk
                    tr(b.v(c0=kk * 128, c1=(kk + 1) * 128, p1=nt), yt.v(k, None, 0, nt), IDENT.v())
                P.add("act", lambda e, o=s_.v(c0=half * 512, c1=half * 512 + 512, p1=nt).ap, i_=b.v(p1=nt).ap: e.copy(o, i_),
                      reads=[b.v()], writes=[s_.v(c0=half * 512, c1=half * 512 + 512)])
            if c0 < TP:
                dst = yp[p * TP + c0:p * TP + c0 + nt, :]
            else:
                dst = ys[p, 0:nt, :]
            store_ops.append(dma("sp", key, [(dst, s_.v(c0=0, c1=1024, p1=nt).ap)], reads=[s_.v()]))

        NT_ = TP // 128 + 1
        for i in range(NT_):
            load_dma(0, i)
            load_xpose(0, i)
        load_caches(0)
        for p in range(npass):
            seqslots = (0, 1 + p)
            for l in range(nlayer):
                layer(p, l, seqslots, presq=(l > 0))
            rs = rms_all(PA, True)
            nxt = p + 1 < npass
            if nxt:
                load_dma(p + 1, 0)
            for i in range(NT_):
                if nxt and i + 1 < NT_:
                    load_dma(p + 1, i + 1)
                out_tile(p, i, rs)
                if nxt:
                    load_xpose(p + 1, i)
            outs = [(HS, l, sts[l, p]) for l in range(nlayer)]
            if p == npass - 1:
                outs += [(HP, l, stp[l]) for l in range(nlayer)]
            for (HB, l, dst) in outs:
                s_, key = next_ostg()
                for half in range(2):
                    b = bank()
                    for kk in range(4):
                        k = half * 4 + kk
                        tr(b.v(c0=kk * 128, c1=(kk + 1) * 128, p1=HR), HB.v(l * KC + k, None), IDENT.v())
                    P.add("act", lambda e, o=s_.v(c0=half * 512, c1=half * 512 + 512, p1=HR).ap, i_=b.v(p1=HR).ap: e.copy(o, i_),
                          reads=[b.v()], writes=[s_.v(c0=half * 512, c1=half * 512 + 512)])
                store_ops.append(dma("sp", key, [(dst, s_.v(c0=0, c1=1024, p1=HR).ap)], reads=[s_.v()]))
            if nxt:
                load_caches(p + 1)

        P.add("sp", None, extra_deps=store_ops)

        sem_names = ["pe", "act", "dve", "pool"]
        dkeys = ["WR%d" % i for i in range(wring_units)] + ["STG0", "STG1", "OSTG0", "OSTG1"]
        sems = {}
        dsems = {}
        for n_ in sem_names:
            sems[n_] = es.enter_context(nc.semaphore("s_" + n_))
        for k_ in dkeys:
            dsems[k_] = es.enter_context(nc.semaphore("d_" + k_))
        block = es.enter_context(nc.Block())
        mk = P.finalize_and_emit(nc, None, sems, dsems)
        block.sync(mk("sp"))
        block.gpsimd(mk("pool"))
        block.tensor(mk("pe"))
        block.scalar(mk("act"))
        block.vector(mk("dve"))
    return nc, P


def _unit(W, c0, r0=0, nk=8):
    blk = W[r0:r0 + nk * 128, c0:c0 + 128].reshape(nk, 128, 128).transpose(1, 0, 2).reshape(128, nk * 128)
    return blk


def prep_shared(inp):
    wall = np.zeros((NL * U_PER_LAYER, 128, 1024), np.float32)
    wada = np.zeros((NL * 48, 128, 1024), np.float32)
    vecs = np.zeros((V_ROWS, 128), np.float32)
    for l in range(NL):
        base = l * U_PER_LAYER
        w_in = inp["w_in"][l]
        splits = {"bg": 0, "cg": 1, "ha": 2, "ga": 3, "gb": 4, "pin": 5}
        for cb in range(KC):
            for j, nm in enumerate(("bg", "cg", "ha", "ga", "gb", "pin")):
                wall[base + U_CB + cb * 6 + j] = _unit(w_in, splits[nm] * 1024 + cb * 128)
        for ob in range(KC):
            wall[base + U_A2 + ob * 2] = _unit(inp["w_out_a"][l], ob * 128)
            wall[base + U_A2 + ob * 2 + 1] = _unit(w_in, 6 * 1024 + ob * 128)
            wall[base + U_B2 + ob * 2] = _unit(inp["w_out_b"][l], ob * 128)
            wall[base + U_B2 + ob * 2 + 1] = _unit(w_in, 7 * 1024 + ob * 128)
            g = ob // 2
            wall[base + U_C2 + ob * 2, :, 0:256] = _unit(inp["w_pool"][l, g], (ob % 2) * 128, nk=2)
            wall[base + U_C2 + ob * 2 + 1] = _unit(w_in, 8 * 1024 + ob * 128)
            wall[base + U_O + ob] = _unit(inp["w_o"][l], ob * 128)
            wfo = inp["w_ffn_out"][l]
            wall[base + U_FO + ob * 3] = _unit(wfo, ob * 128, 0, 8)
            wall[base + U_FO + ob * 3 + 1] = _unit(wfo, ob * 128, 1024, 8)
            wall[base + U_FO + ob * 3 + 2, :, 0:768] = _unit(wfo, ob * 128, 2048, 6)
        wcb = inp["w_conv_b"][l]
        ar = np.arange(128)
        for cb in range(KC):
            for j in range(31):
                u = base + U_CV + cb * 4 + j // 8
                blk = wall[u].reshape(128, 8, 128)
                blk[ar, j % 8, ar] = wcb[j, cb * 128:(cb + 1) * 128]
        wfi = inp["w_ffn_in"][l]
        for fb in range(NFB):
            wall[base + U_FI + fb * 2] = _unit(wfi, fb * 128)
            wall[base + U_FI + fb * 2 + 1] = _unit(wfi, DFF + fb * 128)
        for j in range(48):
            wada[l * 48 + j] = _unit(inp["w_ada"][l], j * 128)
        vb = l * V_PER_LAYER
        vecs[vb + V_N1:vb + V_N1 + 8] = inp["norm1_g"][l].reshape(8, 128)
        vecs[vb + V_N2:vb + V_N2 + 8] = inp["norm2_g"][l].reshape(8, 128)
        vecs[vb + V_CA:vb + V_CA + 24] = inp["w_conv_a"][l].reshape(24, 128)
        vecs[vb + V_CB:vb + V_CB + 248] = inp["w_conv_b"][l].reshape(248, 128)
        vecs[vb + V_BB:vb + V_BB + 8] = inp["b_conv_b"][l].reshape(8, 128)
        vecs[vb + V_LG:vb + V_LG + 8] = inp["ln_b_g"][l].reshape(8, 128)
        vecs[vb + V_LB:vb + V_LB + 8] = inp["ln_b_b"][l].reshape(8, 128)
        vecs[vb + V_PS:vb + V_PS + 8] = inp["pool_scale"][l].reshape(8, 128)
        vecs[vb + V_BA:vb + V_BA + 48] = inp["b_ada"][l].reshape(48, 128)
    vecs[V_FG:V_FG + 8] = inp["final_g"].reshape(8, 128)
    return wall, wada, vecs


def prep_core(inp, c):
    cache = np.concatenate([inp["cache_conv_a"], inp["cache_conv_b"], inp["cache_pool"]], axis=2)
    return {
        "xp": np.ascontiguousarray(inp["x_prompt"][c]),
        "xs": np.ascontiguousarray(inp["x_sample"][4 * c:4 * c + 4]),
        "cvec": np.ascontiguousarray(np.concatenate([inp["c_prompt"][c:c + 1], inp["c_sample"][4 * c:4 * c + 4]], axis=0)),
        "cache": np.ascontiguousarray(cache[:, 4 * c:4 * c + 4]),
    }


_CACHE = {}


def kernel(**inputs):
    inp = {k: np.asarray(v, dtype=np.float32) for k, v in inputs.items()}
    wall, wada, vecs = prep_shared(inp)
    if "nc" not in _CACHE:
        _CACHE["nc"] = build_program()[0]
    nc = _CACHE["nc"]
    in_maps = []
    for c in range(NCORES):
        m = prep_core(inp, c)
        m.update({"wall": wall, "wada": wada, "vecs": vecs})
        in_maps.append(m)
    res = run_bass_kernel_spmd(nc, in_maps, core_ids=list(range(NCORES)))
    R = res.results
    y_prompt = np.stack([R[c]["yp"] for c in range(NCORES)], axis=0).astype(np.float32)
    y_sample = np.concatenate([R[c]["ys"] for c in range(NCORES)], axis=0).astype(np.float32)
    stp = np.stack([R[c]["stp"] for c in range(NCORES)], axis=1)
    sts = np.concatenate([R[c]["sts"] for c in range(NCORES)], axis=1)
    return (y_prompt, y_sample,
            np.ascontiguousarray(stp[:, :, 0:2]), np.ascontiguousarray(stp[:, :, 2:32]), np.ascontiguousarray(stp[:, :, 32:47]),
            np.ascontiguousarray(sts[:, :, 0:2]), np.ascontiguousarray(sts[:, :, 2:32]), np.ascontiguousarray(sts[:, :, 32:47]))
```

```python
import numpy as np
import concourse.bass as bass
import concourse.mybir as mybir
from concourse.bass_utils import run_bass_kernel_spmd

F32 = mybir.dt.float32
BF16 = mybir.dt.bfloat16
AF = mybir.ActivationFunctionType
ALU = mybir.AluOpType

D = 1024
KC = 8
TP = 1024
TS = 64
T = TP + TS
NPASS = 4
NL = 2
DFF = 2816
NFB = DFF // 128
NIN = 9216
EPS = 1e-6
NCORES = 8
HR = 47
SUBT = [(0, 512, 0), (512, 512, 0), (1024, 64, 1)]
POOLW = (2, 4, 8, 16)
TD = 10

U_CB = 0
U_A2 = 48
U_B2 = 64
U_C2 = 80
U_O = 96
U_FI = 104
U_FO = 148
U_CV = 172
U_PER_LAYER = 204

V_N1 = 0
V_N2 = 8
V_CA = 16
V_CB = 40
V_BB = 288
V_LG = 296
V_LB = 304
V_PS = 312
V_BA = 320
V_PER_LAYER = 368
V_FG = 2 * V_PER_LAYER
V_ROWS = 768


class View:
    __slots__ = ("ap", "root", "ivs")

    def __init__(self, ap, root, ivs):
        self.ap = ap
        self.root = root
        self.ivs = ivs


class Buf:
    def __init__(self, space, root, tensor_ap, byte_off, dtype, K, C, nparts=128):
        self.space = space
        self.root = root
        self.K = K
        self.C = C
        self.es = 4 if dtype == F32 else 2
        self.byte_off = byte_off
        self.rs = C * self.es
        nb = K * C * self.es
        assert byte_off % 4 == 0 and nb % 4 == 0
        flat = tensor_ap[:, byte_off // 4:(byte_off + nb) // 4]
        if dtype != F32:
            flat = flat.bitcast(dtype)
        self.ap2 = flat
        self.ap3 = flat.rearrange("p (k c) -> p k c", k=K) if K > 1 else None
        self.nbytes = nb

    def v(self, k0=None, k1=None, c0=0, c1=None, p0=0, p1=128):
        if c1 is None:
            c1 = self.C
        if self.K == 1:
            ap = self.ap2[p0:p1, c0:c1]
            ivs = [(self.byte_off + c0 * self.es, self.byte_off + c1 * self.es)]
            return View(ap, (self.space, self.root), ivs)
        if k0 is None:
            k0, k1 = 0, self.K
        if k1 is None:
            ap = self.ap3[p0:p1, k0, c0:c1]
            ks = [k0]
        else:
            ap = self.ap3[p0:p1, k0:k1, c0:c1]
            ks = list(range(k0, k1))
        if c0 == 0 and c1 == self.C:
            ivs = [(self.byte_off + ks[0] * self.rs, self.byte_off + (ks[-1] + 1) * self.rs)]
        else:
            ivs = [(self.byte_off + k * self.rs + c0 * self.es, self.byte_off + k * self.rs + c1 * self.es) for k in ks]
        return View(ap, (self.space, self.root), ivs)


def _overlap(a, b):
    for (l0, h0) in a:
        for (l1, h1) in b:
            if l0 < h1 and l1 < h0:
                return True
    return False


def _covered(inner, outer):
    for (l0, h0) in inner:
        ok = False
        for (l1, h1) in outer:
            if l1 <= l0 and h0 <= h1:
                ok = True
                break
        if not ok:
            return False
    return True


class Op:
    __slots__ = ("eng", "emit", "deps", "sig", "sem", "ticket", "is_dma", "dkey", "ndma", "idx")


class Prog:
    COMPUTE = ("pe", "act", "dve", "pool")

    def __init__(self, same_engine_sync=True):
        self.ops = []
        self.acc = {}
        self.same_engine_sync = same_engine_sync

    def add(self, eng, emit, reads=(), writes=(), dkey=None, ndma=0, extra_deps=()):
        op = Op()
        op.eng = eng
        op.emit = emit
        op.is_dma = dkey is not None
        op.dkey = dkey
        op.ndma = ndma
        op.sig = op.is_dma
        op.idx = len(self.ops)
        deps = set(extra_deps)
        for r in reads:
            lst = self.acc.setdefault(r.root, [])
            for e in lst:
                if e[3] and _overlap(e[2], r.ivs):
                    deps.add(e[0])
            if not op.is_dma:
                lst[:] = [e for e in lst if e[0] == op.idx or not ((not e[3]) and e[1] == eng and e[2] == r.ivs and not self.ops[e[0]].is_dma)]
            lst.append([op.idx, eng, r.ivs, False])
        for w in writes:
            lst = self.acc.setdefault(w.root, [])
            keep = []
            for e in lst:
                if e[0] == op.idx:
                    keep.append(e)
                    continue
                if _overlap(e[2], w.ivs):
                    deps.add(e[0])
                    if _covered(e[2], w.ivs):
                        continue
                keep.append(e)
            keep.append([op.idx, eng, w.ivs, True])
            lst[:] = keep
        deps.discard(op.idx)
        op.deps = deps
        self.ops.append(op)
        return op.idx

    def finalize_and_emit(self, nc, block_engines, sems, dsems):
        ops = self.ops
        for op in ops:
            for d in op.deps:
                x = ops[d]
                if x.is_dma:
                    continue
                if x.eng == op.eng and not op.is_dma:
                    if op.eng == "pe" or not self.same_engine_sync:
                        continue
                x.sig = True
        cnt = {e: 0 for e in self.COMPUTE}
        dcnt = {}
        for op in ops:
            if op.is_dma:
                dcnt[op.dkey] = dcnt.get(op.dkey, 0) + 16 * op.ndma
                op.sem = dsems[op.dkey]
                op.ticket = dcnt[op.dkey]
            elif op.sig:
                cnt[op.eng] += 1
                op.sem = sems[op.eng]
                op.ticket = cnt[op.eng]
        streams = {}
        for op in ops:
            streams.setdefault(op.eng, []).append(op)
        self.stats = {e: len(s) for e, s in streams.items()}
        self.sigcnt = cnt

        def make_stream(eng_name):
            def fn(e):
                waited = {}
                for op in streams.get(eng_name, []):
                    need = {}
                    for d in op.deps:
                        x = ops[d]
                        if not x.is_dma and x.eng == op.eng and not op.is_dma:
                            if op.eng == "pe" or not self.same_engine_sync:
                                continue
                        key = id(x.sem)
                        if key not in need or need[key][1] < x.ticket:
                            need[key] = (x.sem, x.ticket)
                    for key, (sem, val) in need.items():
                        if waited.get(key, 0) < val:
                            e.wait_ge(sem, val)
                            waited[key] = val
                    if op.emit is not None:
                        if op.is_dma:
                            op.emit(e, op.sem)
                        else:
                            ins = op.emit(e)
                            if op.sig:
                                ins.then_inc(op.sem, 1)
            return fn

        return make_stream


def build_program(npass=NPASS, nlayer=NL, same_engine_sync=True, wring_units=11):
    nc = bass.Bass("TRN2", target_bir_lowering=False)
    ntok_p = npass * TP
    xp = nc.dram_tensor("xp", [NPASS * TP, D], F32, kind="ExternalInput").ap()
    xs = nc.dram_tensor("xs", [NPASS, TS, D], F32, kind="ExternalInput").ap()
    cvec = nc.dram_tensor("cvec", [5, D], F32, kind="ExternalInput").ap()
    cache = nc.dram_tensor("cache", [NL, NPASS, HR, D], F32, kind="ExternalInput").ap()
    vecs = nc.dram_tensor("vecs", [V_ROWS, 128], F32, kind="ExternalInput").ap()
    wall = nc.dram_tensor("wall", [NL * U_PER_LAYER, 128, 1024], F32, kind="ExternalInput").ap()
    wada = nc.dram_tensor("wada", [NL * 48, 128, 1024], F32, kind="ExternalInput").ap()
    yp = nc.dram_tensor("yp", [NPASS * TP, D], F32, kind="ExternalOutput").ap()
    ys = nc.dram_tensor("ys", [NPASS, TS, D], F32, kind="ExternalOutput").ap()
    stp = nc.dram_tensor("stp", [NL, HR, D], F32, kind="ExternalOutput").ap()
    sts = nc.dram_tensor("sts", [NL, NPASS, HR, D], F32, kind="ExternalOutput").ap()

    P = Prog(same_engine_sync=same_engine_sync)

    off = [0]

    def alloc(nbytes):
        o = off[0]
        off[0] += (nbytes + 3) // 4 * 4
        return o

    o_X = alloc(KC * T * 4)
    o_H = alloc(KC * T * 2)
    o_PA = alloc(KC * T * 2)
    o_PC = alloc(KC * T * 2)
    o_BIG = alloc(KC * T * 4)
    o_PB = alloc(KC * T * 2)
    o_WR = alloc(wring_units * 2048)
    CU = 2 + TP + 2 + TS
    CV = 30 + TP + 30 + TS
    CP = 15 + TP + 15 + TS
    CT = 1152
    o_UF = alloc(CU * 4)
    o_VF = alloc(2 * CV * 2)
    o_PF = alloc(CP * 4)
    o_T1 = alloc(CT * 4)
    o_T2 = alloc(CT * 4)
    o_T3 = alloc(CT * 4)
    o_ST = alloc(3 * 512 * 4)
    o_VEC = alloc(V_ROWS * 4)
    o_ID = alloc(128 * 4)
    o_ONE = alloc(128 * 2)
    o_HP = alloc(NL * KC * HR * 4)
    o_HS = alloc(NL * KC * HR * 4)
    o_ADA = alloc(NL * 48 * 5 * 4)
    o_AA = alloc(NL * 2 * KC * 5 * 4)
    o_SC = alloc(KC * 5 * 2 + 16)
    o_IC = alloc(4 * 16 * 4)
    o_EPS = alloc(4)
    o_BGS = alloc(T * 4)
    arena_bytes = off[0]
    assert arena_bytes <= (nc.sbuf_top - nc.sbuf_base - 64), arena_bytes

    import contextlib
    es = contextlib.ExitStack()
    with es:
        arena_t = es.enter_context(nc.sbuf_tensor("arena", [128, arena_bytes // 4], F32))
        psum_t = es.enter_context(nc.psum_tensor("psum", [128, 8 * 512], F32))
        A = arena_t[:, :]
        PSAP = psum_t[:, :]

        def sb(root, o, dtype, K, C):
            return Buf("sb", root, A, o, dtype, K, C)

        X = sb("X", o_X, F32, KC, T)
        H = sb("H", o_H, BF16, KC, T)
        PA = sb("PA", o_PA, BF16, KC, T)
        PC = sb("PC", o_PC, BF16, KC, T)
        BIG = sb("BIGPB", o_BIG, F32, KC, T)
        PB = sb("BIGPB", o_PB, BF16, KC, T)
        ACTB = sb("BIGPB", o_BIG, BF16, NFB, T)
        WR = [sb("WR%d" % i, o_WR + i * 2048, BF16, KC, 128) for i in range(wring_units)]
        UF = sb("UF", o_UF, F32, 1, CU)
        VFB = [sb("VF%d" % i, o_VF + i * CV * 2, BF16, 1, CV) for i in range(2)]
        PF = sb("PF", o_PF, F32, 1, CP)
        SQ2 = sb("UVP", o_UF, BF16, KC, 512)
        assert o_VF == o_UF + CU * 4 and o_PF == o_VF + 2 * CV * 2 and CU * 4 + 2 * CV * 2 >= KC * 512 * 2
        OSTG = [sb("UVP", o_UF, F32, 1, 1024), sb("UVP", o_PF, F32, 1, 1024)]
        SQH = sb("H", o_H, BF16, KC, 512)
        for b_ in (UF, PF, VFB[0], VFB[1]):
            b_.root = "UVP"
        T1 = sb("T1", o_T1, F32, 1, CT)
        T2 = sb("T2", o_T2, F32, 1, CT)
        T3 = sb("T3", o_T3, F32, 1, CT)
        SQB = sb("T1", o_T1, BF16, KC, 512)
        STG = [sb("T1", o_T1, F32, 1, 1024), sb("T2", o_T2 + 0, F32, 1, 1024)]
        STAT = sb("STAT", o_ST, F32, 3, 512)
        VEC = sb("VEC", o_VEC, F32, 1, V_ROWS)
        IDENT = sb("ID", o_ID, F32, 1, 128)
        ONES = sb("ONE", o_ONE, BF16, 1, 128)
        HP = sb("HP", o_HP, F32, NL * KC, HR)
        HS = sb("HS", o_HS, F32, NL * KC, HR)
        ADA = sb("ADA", o_ADA, F32, NL * 48, 5)
        AA = sb("AA", o_AA, F32, NL * 2 * KC, 5)
        SC = sb("SC", o_SC, BF16, KC, 5)
        IC = sb("IC", o_IC, F32, 4, 16)
        EPSC = sb("EPSC", o_EPS, F32, 1, 1)
        BGS = sb("BGS", o_BGS, F32, 1, T)
        for b in (T1, T2, SQB, STG[0], STG[1]):
            b.root = "T12"
        assert CT * 4 >= 4096 and o_T2 == o_T1 + CT * 4
        assert 2 * CT * 4 >= KC * 512 * 2

        PSB = [Buf("ps", "PS%d" % i, PSAP, i * 2048, F32, 1, 512) for i in range(8)]
        psi = [0]

        def bank():
            b = PSB[psi[0] % 8]
            psi[0] += 1
            return b

        def mm(out, lhsT, rhs, start, stop):
            P.add("pe", lambda e, o=out.ap, l=lhsT.ap, r=rhs.ap, s=start, t=stop: e.matmul(o, l, r, start=s, stop=t),
                  reads=[lhsT, rhs], writes=[out])

        def tr(out, in_, ident):
            P.add("pe", lambda e, o=out.ap, i=in_.ap, d=ident.ap: e.transpose(o, i, d), reads=[in_, ident], writes=[out])

        def act(out, in_, func, scale=1.0, bias=0.0, eng="act"):
            rd = [in_]
            sc = scale
            bi = bias
            if isinstance(scale, View):
                rd.append(scale)
                sc = scale.ap
            if isinstance(bias, View):
                rd.append(bias)
                bi = bias.ap
            P.add("act", lambda e, o=out.ap, i=in_.ap, f=func, s=sc, b=bi: e.activation(o, i, f, bias=b, scale=s),
                  reads=rd, writes=[out])

        def tt(out, a, b, op, eng="dve"):
            P.add(eng, lambda e, o=out.ap, x=a.ap, y=b.ap, p=op: e.tensor_tensor(o, x, y, p), reads=[a, b], writes=[out])

        def ts(out, a, s1, s2, op0, op1=None, eng="dve"):
            rd = [a]
            v1, v2 = s1, s2
            if isinstance(s1, View):
                rd.append(s1)
                v1 = s1.ap
            if isinstance(s2, View):
                rd.append(s2)
                v2 = s2.ap
            if op1 is None:
                P.add(eng, lambda e, o=out.ap, x=a.ap, u=v1, p=op0: e.tensor_scalar(o, x, u, None, p), reads=rd, writes=[out])
            else:
                P.add(eng, lambda e, o=out.ap, x=a.ap, u=v1, w=v2, p=op0, q=op1: e.tensor_scalar(o, x, u, w, p, q),
                      reads=rd, writes=[out])

        def stt(out, in0, scalar, in1, op0, op1, eng="dve"):
            rd = [in0, in1]
            sv = scalar
            if isinstance(scalar, View):
                rd.append(scalar)
                sv = scalar.ap
            P.add(eng, lambda e, o=out.ap, x=in0.ap, s=sv, y=in1.ap, p=op0, q=op1: e.scalar_tensor_tensor(o, x, s, y, p, q),
                  reads=rd, writes=[out])

        def powm05(v):
            P.add("pool", lambda e, a=v.ap: e.tensor_single_scalar(a, a, -0.5, ALU.pow), reads=[v], writes=[v])

        def cp(out, in_, eng="dve"):
            P.add(eng, lambda e, o=out.ap, i=in_.ap: e.tensor_copy(o, i), reads=[in_], writes=[out])

        def dma(queue, dkey, pairs, reads=(), writes=()):
            def emit(e, sem, pairs=pairs):
                for (o, i) in pairs:
                    e.dma_start(out=o, in_=i).then_inc(sem, 16)
            return P.add(queue, emit, reads=reads, writes=writes, dkey=dkey, ndma=len(pairs))

        def vcol(row):
            return VEC.v(c0=row, c1=row + 1)

        wr_next = [0]

        def load_unit(src_ap_full, nk=8):
            slot = WR[wr_next[0] % wring_units]
            key = "WR%d" % (wr_next[0] % wring_units)
            wr_next[0] += 1
            dst = slot.v(0, nk)
            dma("pool", key, [(dst.ap, src_ap_full[:, 0:nk * 128].rearrange("p (k n) -> p k n", k=nk))], writes=[dst])
            return slot

        pending = []
        loaded = []

        class WStream:
            def __init__(self):
                self.sched = []
                self.issued = 0
                self.slots = {}

            def plan(self, src, nk=8):
                self.sched.append((src, nk))
                return len(self.sched) - 1

        DEFER_ADA1 = nlayer > 1

        def sched_for(p):
            lst = []
            for l in range(nlayer):
                base = l * U_PER_LAYER
                for cb in range(KC + 1):
                    if cb < KC:
                        for j in (0, 1, 2, 5):
                            lst.append(("wall", base + U_CB + cb * 6 + j, 8))
                    if cb >= 1:
                        for u in range(TD // 8, 4):
                            lst.append(("wall", base + U_CV + (cb - 1) * 4 + u, 8 if u < 3 else 7))
                    if cb < KC:
                        lst.append(("wall", base + U_CB + cb * 6 + 3, 8))
                        lst.append(("wall", base + U_CB + cb * 6 + 4, 8))
                        if DEFER_ADA1 and p == 0 and l == 0:
                            for i in range(6):
                                lst.append(("wada", 48 + cb * 6 + i, 8))
                for u in range(U_A2, U_CV):
                    nk = 8
                    if U_C2 <= u < U_O and (u - U_C2) % 2 == 0:
                        nk = 2
                    if U_FO <= u and (u - U_FO) % 3 == 2:
                        nk = 6
                    lst.append(("wall", base + u, nk))
            return lst

        wsched = []
        for p_ in range(npass):
            wsched += sched_for(p_)
        wpos = {"issued": 0, "used": 0, "slots": {}}
        total_units = [0]

        def w_prefetch(upto):
            while wpos["issued"] < upto and wpos["issued"] < total_units[0]:
                g = wpos["issued"]
                (tn, ui, nk) = wsched[g]
                wpos["slots"][g] = load_unit(wall[ui] if tn == "wall" else wada[ui], nk)
                wpos["issued"] += 1

        def w_take(n):
            g = wpos["used"]
            w_prefetch(g + wring_units)
            wpos["used"] += n
            return [wpos["slots"].pop(g + i) for i in range(n)]

        def w_next():
            return w_take(1)[0]

        total_units[0] = len(wsched)

        P.add("pool", lambda e: e.memset(IDENT.v().ap, 0.0), writes=[IDENT.v()])
        P.add("pool", lambda e: e.iota(IDENT.v().ap, [[1, 128]], channel_multiplier=-1, allow_small_or_imprecise_dtypes=True),
              writes=[IDENT.v()], reads=[IDENT.v()])
        P.add("pool", lambda e: e.tensor_single_scalar(IDENT.v().ap, IDENT.v().ap, 0.0, ALU.is_equal),
              reads=[IDENT.v()], writes=[IDENT.v()])
        P.add("pool", lambda e: e.memset(ONES.v().ap, 1.0), writes=[ONES.v()])
        P.add("pool", lambda e: e.memset(EPSC.v().ap, EPS), writes=[EPSC.v()])
        P.add("pool", lambda e: e.memset(HP.v().ap, 0.0), writes=[HP.v()])
        for g, w in enumerate(POOLW):
            for j in range(15):
                val = 1.0 / min(j + 1, w)
                P.add("pool", lambda e, a=IC.v(g, None, j, j + 1).ap, v=val: e.memset(a, v), writes=[IC.v(g, None, j, j + 1)])

        for i in range(V_ROWS // 128):
            s = STG[i % 2]
            dma("sp", "STG%d" % (i % 2), [(s.v(c0=0, c1=128).ap, vecs[i * 128:(i + 1) * 128, :])], writes=[s.v(c0=0, c1=128)])
            b = bank()
            tr(b.v(c0=0, c1=128), s.v(c0=0, c1=128), IDENT.v())
            cp(VEC.v(c0=i * 128, c1=(i + 1) * 128), b.v(c0=0, c1=128), eng="dve")
        s = STG[0]
        dma("sp", "STG0", [(s.v(c0=0, c1=1024, p1=5).ap, cvec[:, :])], writes=[s.v()])
        b = bank()
        for k in range(KC):
            tr(b.v(c0=k * 5, c1=k * 5 + 5), s.v(c0=k * 128, c1=(k + 1) * 128, p1=5), IDENT.v(c0=0, c1=5, p1=5))
        act(T3.v(c0=0, c1=40), b.v(c0=0, c1=40), AF.Sigmoid)
        P.add("dve", lambda e, o=SC.ap2[:, 0:40], x=T3.v(c0=0, c1=40).ap, y=b.v(c0=0, c1=40).ap: e.tensor_tensor(o, x, y, ALU.mult),
              reads=[T3.v(c0=0, c1=40), b.v(c0=0, c1=40)], writes=[SC.v()])

        def ada_evac(l, b, j0, nj):
            adal = Buf("sb", "ADA", A, o_ADA + l * 48 * 5 * 4, F32, 48, 5)
            r0 = l * V_PER_LAYER + V_BA + j0
            bview = View(VEC.ap2[:, r0:r0 + nj].unsqueeze(2).to_broadcast([128, nj, 5]), VEC.v().root, VEC.v().ivs)
            pview = View(b.ap2[:, 0:nj * 5].rearrange("p (j s) -> p j s", j=nj), b.v().root, b.v(c0=0, c1=nj * 5).ivs)
            tt(adal.v(j0, j0 + nj), pview, bview, ALU.add)

        def ada_derive(l):
            adal = Buf("sb", "ADA", A, o_ADA + l * 48 * 5 * 4, F32, 48, 5)
            for which, (vrow, aoff) in enumerate(((V_N1, 8), (V_N2, 32))):
                aab = Buf("sb", "AA", A, o_AA + (l * 2 + which) * KC * 5 * 4, F32, KC, 5)
                gview = View(VEC.ap2[:, l * V_PER_LAYER + vrow:l * V_PER_LAYER + vrow + 8].unsqueeze(2).to_broadcast([128, 8, 5]),
                             VEC.v().root, VEC.v().ivs)
                stt(aab.v(), adal.v(aoff, aoff + 8), 1.0, gview, ALU.add, ALU.mult)

        for l in range(1 if DEFER_ADA1 else nlayer):
            b = bank()
            for j in range(48):
                slot = WR[wr_next[0] % wring_units]
                key = "WR%d" % (wr_next[0] % wring_units)
                wr_next[0] += 1
                dst = slot.v(0, 8)
                dma("pool", key, [(dst.ap, wada[l * 48 + j].rearrange("p (k n) -> p k n", k=8))], writes=[dst])
                for k in range(KC):
                    mm(b.v(c0=j * 5, c1=j * 5 + 5), slot.v(k, None), SC.v(k, None), k == 0, k == KC - 1)
            ada_evac(l, b, 0, 48)
            ada_derive(l)

        def ada1_step(cb):
            us = w_take(6)
            b = bank()
            for i, slot in enumerate(us):
                for k in range(KC):
                    mm(b.v(c0=i * 5, c1=i * 5 + 5), slot.v(k, None), SC.v(k, None), k == 0, k == KC - 1)
            ada_evac(1, b, cb * 6, 6)
            if cb == KC - 1:
                ada_derive(1)

        def ada_col(l, j, s):
            adal = Buf("sb", "ADA", A, o_ADA + l * 48 * 5 * 4, F32, 48, 5)
            return adal.v(j, None, s, s + 1)

        def aa_col(l, which, k, s):
            aab = Buf("sb", "AA", A, o_AA + (l * 2 + which) * KC * 5 * 4, F32, KC, 5)
            return aab.v(k, None, s, s + 1)

        def hist(HB, l, cb, c0, c1):
            return HB.v(l * KC + cb, None, c0, c1)

        def rms_all(sqb, presq=False):
            rs = []
            for si, (c0, n, sq_i) in enumerate(SUBT):
                b = bank()
                for k in range(KC):
                    if presq:
                        sq = sqb.v(k, None, c0, c0 + n)
                    else:
                        sq = sqb.v(k, None, 0, n)
                        xv = X.v(k, None, c0, c0 + n)
                        if k % 2 == 0:
                            act(sq, xv, AF.Square)
                        else:
                            tt(sq, xv, xv, ALU.mult)
                    mm(b.v(c0=0, c1=n), ONES.v(), sq, k == 0, k == KC - 1)
                r = STAT.v(si, None, 0, n)
                act(r, b.v(c0=0, c1=n), AF.Sqrt, scale=1.0 / D, bias=EPSC.v())
                P.add("dve", lambda e, a=r.ap: e.reciprocal(a, a), reads=[r], writes=[r])
                rs.append(r)
            return rs

        def ada_norm(l, which, seqslots, presq=False):
            shoff = 0 if which == 0 else 24
            rs = rms_all(PA, True) if presq else rms_all(SQB)
            for si, (c0, n, sq_i) in enumerate(SUBT):
                s = seqslots[sq_i]
                for k in range(KC):
                    tmp = T3.v(c0=(k % 2) * 512, c1=(k % 2) * 512 + n)
                    tt(tmp, X.v(k, None, c0, c0 + n), rs[si], ALU.mult)
                    act(H.v(k, None, c0, c0 + n), tmp, AF.Identity, scale=aa_col(l, which, k, s), bias=ada_col(l, shoff + k, s))

        def norm_st(l, which, seqslots, si, sqb, part="ABC"):
            shoff = 0 if which == 0 else 24
            (c0, n, sq_i) = SUBT[si]
            if "A" in part:
                for k in range(KC):
                    sq = sqb.v(k, None, 0, n)
                    xv = X.v(k, None, c0, c0 + n)
                    if k % 2 == 0:
                        act(sq, xv, AF.Square)
                    else:
                        tt(sq, xv, xv, ALU.mult)
            r = STAT.v(si, None, 0, n)
            if "B" in part:
                b = bank()
                for k in range(KC):
                    mm(b.v(c0=0, c1=n), ONES.v(), sqb.v(k, None, 0, n), k == 0, k == KC - 1)
                act(r, b.v(c0=0, c1=n), AF.Sqrt, scale=1.0 / D, bias=EPSC.v())
                P.add("dve", lambda e, a=r.ap: e.reciprocal(a, a), reads=[r], writes=[r])
            if "C" in part:
                s_ = seqslots[sq_i]
                for k in range(KC):
                    tmp = T3.v(c0=(k % 2) * 512, c1=(k % 2) * 512 + n)
                    tt(tmp, X.v(k, None, c0, c0 + n), r, ALU.mult)
                    act(H.v(k, None, c0, c0 + n), tmp, AF.Identity, scale=aa_col(l, which, k, s_), bias=ada_col(l, shoff + k, s_))

        def conv_b(l, cbp):
            vbq = l * V_PER_LAYER
            u0 = TD // 8
            dv = w_take(4 - u0)
            vfbp = VFB[cbp % 2]
            for (c0, n, sq_i) in SUBT:
                bk = bank()
                off = c0 if sq_i == 0 else TP + 30 + (c0 - TP)
                for j in range(TD, 31):
                    mm(bk.v(c0=0, c1=n), dv[j // 8 - u0].v(j % 8, None), vfbp.v(c0=off + j, c1=off + j + n), j == TD, j == 30)
                bv = BIG.v(cbp, None, c0, c0 + n)
                stt(bv, bk.v(c0=0, c1=n), vcol(vbq + V_BB + cbp), bv, ALU.add, ALU.add)

        def layer(p, l, seqslots, presq=False):
            vb = l * V_PER_LAYER
            first_prompt = (p == 0)
            deferred = []

            def drain(n=1):
                for _ in range(min(n, len(deferred))):
                    deferred.pop(0)()

            ada_norm(l, 0, seqslots, presq)
            for cb in range(KC):
                wbg, wcg, wha, wpin = w_take(4)
                bgb = []
                for (c0, n, sq_i) in SUBT:
                    bc, bh, bb = bank(), bank(), bank()
                    for k in range(KC):
                        mm(bc.v(c0=0, c1=n), wcg.v(k, None), H.v(k, None, c0, c0 + n), k == 0, k == KC - 1)
                    for k in range(KC):
                        mm(bh.v(c0=0, c1=n), wha.v(k, None), H.v(k, None, c0, c0 + n), k == 0, k == KC - 1)
                    for k in range(KC):
                        mm(bb.v(c0=0, c1=n), wbg.v(k, None), H.v(k, None, c0, c0 + n), k == 0, k == KC - 1)
                    act(T3.v(c0=c0, c1=c0 + n), bc.v(c0=0, c1=n), AF.Copy)
                    act(BGS.v(c0=c0, c1=c0 + n), bb.v(c0=0, c1=n), AF.Copy)
                    uoff = 2 + c0 if sq_i == 0 else 2 + TP + 2 + (c0 - TP)
                    tt(UF.v(c0=uoff, c1=uoff + n), bh.v(c0=0, c1=n), T3.v(c0=c0, c1=c0 + n), ALU.mult)
                    drain(1)
                cp(UF.v(c0=0, c1=2), hist(HP, l, cb, 0, 2), eng="pool")
                cp(UF.v(c0=2 + TP, c1=4 + TP), hist(HS, l, cb, 0, 2), eng="pool")
                NO = TP + 2 + TS
                wa = [vcol(vb + V_CA + j * 8 + cb) for j in range(3)]
                ts(T2.v(c0=0, c1=NO), UF.v(c0=0, c1=NO), wa[0], None, ALU.mult)
                stt(T2.v(c0=0, c1=NO), UF.v(c0=1, c1=1 + NO), wa[1], T2.v(c0=0, c1=NO), ALU.mult, ALU.add)
                stt(T2.v(c0=0, c1=NO), UF.v(c0=2, c1=2 + NO), wa[2], T2.v(c0=0, c1=NO), ALU.mult, ALU.add)
                drain(1)
                for i, (c0, n, sq_i) in enumerate(SUBT):
                    toff = c0 if sq_i == 0 else TP + 2 + (c0 - TP)
                    tt(PA.v(cb, None, c0, c0 + n), T2.v(c0=toff, c1=toff + n), BGS.v(c0=c0, c1=c0 + n), ALU.mult)
                    drain(1)
                cp(hist(HP, l, cb, 0, 2), UF.v(c0=TP, c1=TP + 2), eng="pool")
                cp(hist(HS, l, cb, 0, 2), UF.v(c0=CU - 2, c1=CU), eng="pool")
                for (c0, n, sq_i) in SUBT:
                    bp = bank()
                    for k in range(KC):
                        mm(bp.v(c0=0, c1=n), wpin.v(k, None), H.v(k, None, c0, c0 + n), k == 0, k == KC - 1)
                    poff = 15 + c0 if sq_i == 0 else 15 + TP + 15 + (c0 - TP)
                    act(PF.v(c0=poff, c1=poff + n), bp.v(c0=0, c1=n), AF.Copy)
                cp(PF.v(c0=0, c1=15), hist(HP, l, cb, 32, 47), eng="pool")
                cp(PF.v(c0=15 + TP, c1=30 + TP), hist(HS, l, cb, 32, 47), eng="pool")
                g = cb // 2
                w = POOLW[g]
                src = PF
                sh = 1
                tbufs = [T2, T3]
                ti = 0
                while sh < w:
                    dst = tbufs[ti % 2]
                    ti += 1
                    tt(dst.v(c0=sh, c1=CP), src.v(c0=sh, c1=CP), src.v(c0=0, c1=CP - sh), ALU.add)
                    src = dst
                    sh *= 2
                for (c0, n, sq_i) in SUBT:
                    poff = 15 + c0 if sq_i == 0 else 15 + TP + 15 + (c0 - TP)
                    stt(PC.v(cb, None, c0, c0 + n), src.v(c0=poff, c1=poff + n), 1.0 / w, PF.v(c0=poff, c1=poff + n),
                        ALU.mult, ALU.subtract)
                    drain(1)
                if first_prompt:
                    tmpv = UF.v(c0=0, c1=15)
                    tt(tmpv, src.v(c0=15, c1=30), IC.v(g, None, 0, 15), ALU.mult)
                    tt(PC.v(cb, None, 0, 15), tmpv, PF.v(c0=15, c1=30), ALU.subtract)
                cp(hist(HP, l, cb, 32, 47), PF.v(c0=TP, c1=TP + 15), eng="pool")
                cp(hist(HS, l, cb, 32, 47), PF.v(c0=CP - 15, c1=CP), eng="pool")
                vfb = VFB[cb % 2]
                drain(100)
                if cb >= 1:
                    conv_b(l, cb - 1)
                wga, wgb = w_take(2)
                cp(T1.v(c0=0, c1=30), hist(HP, l, cb, 2, 32), eng="pool")
                cp(T1.v(c0=30 + TP, c1=60 + TP), hist(HS, l, cb, 2, 32), eng="pool")
                for (c0, n, sq_i) in SUBT:
                    bgt, bga = bank(), bank()
                    for k in range(KC):
                        mm(bgt.v(c0=0, c1=n), wgb.v(k, None), H.v(k, None, c0, c0 + n), k == 0, k == KC - 1)
                    for k in range(KC):
                        mm(bga.v(c0=0, c1=n), wga.v(k, None), H.v(k, None, c0, c0 + n), k == 0, k == KC - 1)
                    voff = 30 + c0 if sq_i == 0 else 30 + TP + 30 + (c0 - TP)
                    vv = T1.v(c0=voff, c1=voff + n)
                    act(vv, bgt.v(c0=0, c1=n), AF.Sigmoid)
                    tt(vv, bga.v(c0=0, c1=n), vv, ALU.mult)
                act(vfb.v(c0=0, c1=CV), T1.v(c0=0, c1=CV), AF.Copy)
                cp(hist(HP, l, cb, 2, 32), T1.v(c0=TP, c1=TP + 30), eng="pool")
                cp(hist(HS, l, cb, 2, 32), T1.v(c0=CV - 30, c1=CV), eng="pool")
                wbv = [vcol(vb + V_CB + j * 8 + cb) for j in range(31)]
                bp_, bs_ = BIG.v(cb, None, 0, TP), BIG.v(cb, None, TP, T)
                so = TP + 30
                def tap0(bp_=bp_, bs_=bs_, w0=wbv[0]):
                    ts(bp_, T1.v(c0=0, c1=TP), w0, None, ALU.mult)
                    ts(bs_, T1.v(c0=so, c1=so + TS), w0, None, ALU.mult)
                deferred.append(tap0)
                for j in range(1, TD):
                    def tapj(j=j, bp_=bp_, bs_=bs_, wj=wbv[j]):
                        stt(bp_, T1.v(c0=j, c1=j + TP), wj, bp_, ALU.mult, ALU.add)
                        stt(bs_, T1.v(c0=so + j, c1=so + j + TS), wj, bs_, ALU.mult, ALU.add)
                    deferred.append(tapj)
                drain(2)
                if DEFER_ADA1 and p == 0 and l == 0:
                    ada1_step(cb)
            drain(100)
            conv_b(l, KC - 1)
            lnst = []
            def srow(si, n):
                if si == 0:
                    return STAT.v(0, None, 0, n), STAT.v(1, None, 0, n)
                if si == 1:
                    return STAT.v(2, None, 0, n), T3.v(c0=0, c1=n)
                return T3.v(c0=512, c1=512 + n), T3.v(c0=576, c1=576 + n)
            for si, (c0, n, sq_i) in enumerate(SUBT):
                b1, b2 = bank(), bank()
                for k in range(KC):
                    cbf = SQB.v(k, None, 0, n)
                    cp(cbf, BIG.v(k, None, c0, c0 + n))
                    mm(b1.v(c0=0, c1=n), ONES.v(), cbf, k == 0, k == KC - 1)
                for k in range(KC):
                    sq = SQ2.v(k, None, 0, n)
                    act(sq, BIG.v(k, None, c0, c0 + n), AF.Square)
                    mm(b2.v(c0=0, c1=n), ONES.v(), sq, k == 0, k == KC - 1)
                mean, rstd = srow(si, n)
                ts(mean, b1.v(c0=0, c1=n), 1.0 / D, None, ALU.mult)
                stt(rstd, mean, -1.0, mean, ALU.mult, ALU.mult)
                stt(rstd, b2.v(c0=0, c1=n), 1.0 / D, rstd, ALU.mult, ALU.add)
                act(rstd, rstd, AF.Sqrt, scale=1.0, bias=EPSC.v())
                P.add("dve", lambda e, a=rstd.ap: e.reciprocal(a, a), reads=[rstd], writes=[rstd])
            def ln_norm(k):
                for si, (c0, n, sq_i) in enumerate(SUBT):
                    mean, rstd = srow(si, n)
                    bv = BIG.v(k, None, c0, c0 + n)
                    tt(bv, bv, mean, ALU.subtract)
                    tt(bv, bv, rstd, ALU.mult)
                    act(PB.v(k, None, c0, c0 + n), bv, AF.Silu, scale=vcol(vb + V_LG + k), bias=vcol(vb + V_LB + k))
            for br in range(3):
                for ob in range(KC):
                    wy, wg = w_take(2)
                    if br == 0:
                        ln_norm(ob)
                    for (c0, n, sq_i) in SUBT:
                        by, bg_ = bank(), bank()
                        if br == 0:
                            for k in range(KC):
                                mm(by.v(c0=0, c1=n), wy.v(k, None), PA.v(k, None, c0, c0 + n), k == 0, k == KC - 1)
                        elif br == 1:
                            for k in range(KC):
                                mm(by.v(c0=0, c1=n), wy.v(k, None), PB.v(k, None, c0, c0 + n), k == 0, k == KC - 1)
                        else:
                            gq = ob // 2
                            for j in range(2):
                                mm(by.v(c0=0, c1=n), wy.v(j, None), PC.v(2 * gq + j, None, c0, c0 + n), j == 0, j == 1)
                        for k in range(KC):
                            mm(bg_.v(c0=0, c1=n), wg.v(k, None), H.v(k, None, c0, c0 + n), k == 0, k == KC - 1)
                        sg = T1.v(c0=c0, c1=c0 + n)
                        act(sg, bg_.v(c0=0, c1=n), AF.Sigmoid)
                        mv = BIG.v(ob, None, c0, c0 + n)
                        if br == 0:
                            tt(mv, by.v(c0=0, c1=n), sg, ALU.mult)
                        elif br == 1:
                            tmp = T2.v(c0=c0, c1=c0 + n)
                            tt(tmp, by.v(c0=0, c1=n), sg, ALU.mult)
                            tt(mv, mv, tmp, ALU.add)
                        else:
                            tmp = T2.v(c0=c0, c1=c0 + n)
                            stt(tmp, by.v(c0=0, c1=n), vcol(vb + V_PS + ob), sg, ALU.mult, ALU.mult)
                            tt(PA.v(ob, None, c0, c0 + n), mv, tmp, ALU.add)
            wos = w_take(8)
            def o_st(si):
                (c0, n, sq_i) = SUBT[si]
                for ob in range(KC):
                    bo = bank()
                    for k in range(KC):
                        mm(bo.v(c0=0, c1=n), wos[ob].v(k, None), PA.v(k, None, c0, c0 + n), k == 0, k == KC - 1)
                    xv = X.v(ob, None, c0, c0 + n)
                    stt(xv, bo.v(c0=0, c1=n), ada_col(l, 16 + ob, seqslots[sq_i]), xv, ALU.mult, ALU.add)

            SQ3 = sb("PC", o_PC, BF16, KC, 512)
            o_st(0)
            norm_st(l, 1, seqslots, 0, SQ2, "A")
            o_st(2)
            norm_st(l, 1, seqslots, 0, SQ2, "BC")
            norm_st(l, 1, seqslots, 2, SQ3, "A")
            o_st(1)
            norm_st(l, 1, seqslots, 2, SQ3, "BC")
            norm_st(l, 1, seqslots, 1, SQ2, "ABC")

            def ffn_in(fb, si, wgt, wup):
                (c0, n, sq_i) = SUBT[si]
                bgt, bup = bank(), bank()
                for k in range(KC):
                    mm(bgt.v(c0=0, c1=n), wgt.v(k, None), H.v(k, None, c0, c0 + n), k == 0, k == KC - 1)
                for k in range(KC):
                    mm(bup.v(c0=0, c1=n), wup.v(k, None), H.v(k, None, c0, c0 + n), k == 0, k == KC - 1)
                gs = T1.v(c0=c0, c1=c0 + n)
                act(gs, bgt.v(c0=0, c1=n), AF.Silu)
                tt(ACTB.v(fb, None, c0, c0 + n), bup.v(c0=0, c1=n), gs, ALU.mult)

            NG = 3
            w6 = w_take(2 * NG)
            for grp in ((0, 2), (1,)):
                for j in range(NG):
                    for si in grp:
                        ffn_in(j, si, w6[2 * j], w6[2 * j + 1])
            for fb in range(NG, NFB):
                wgt, wup = w_take(2)
                for si in range(3):
                    ffn_in(fb, si, wgt, wup)
            for ob in range(KC):
                wf = w_take(3)
                for (c0, n, sq_i) in SUBT:
                    bo = bank()
                    for f in range(NFB):
                        mm(bo.v(c0=0, c1=n), wf[f // 8].v(f % 8, None), ACTB.v(f, None, c0, c0 + n), f == 0, f == NFB - 1)
                    xv = X.v(ob, None, c0, c0 + n)
                    stt(xv, bo.v(c0=0, c1=n), ada_col(l, 40 + ob, seqslots[sq_i]), xv, ALU.mult, ALU.add)
                    if (ob + (0 if sq_i == 0 else 1)) % 2 == 0:
                        act(PA.v(ob, None, c0, c0 + n), xv, AF.Square)
                    else:
                        tt(PA.v(ob, None, c0, c0 + n), xv, xv, ALU.mult)

        store_ops = []
        ostg_i = [0]

        def next_ostg():
            i = ostg_i[0] % 2
            ostg_i[0] += 1
            return OSTG[i], "OSTG%d" % i

        def pass_tiles(p):
            tl = [(xp[p * TP + i * 128:p * TP + (i + 1) * 128, :], 128, i * 128) for i in range(TP // 128)]
            tl.append((xs[p], TS, TP))
            return tl

        def load_dma(p, i):
            (src, nt, c0) = pass_tiles(p)[i]
            s_ = STG[i % 2]
            dma("sp", "STG%d" % (i % 2), [(s_.v(c0=0, c1=1024, p1=nt).ap, src)], writes=[s_.v()])

        def load_xpose(p, i):
            (src, nt, c0) = pass_tiles(p)[i]
            s_ = STG[i % 2]
            for half in range(2):
                b = bank()
                for kk in range(4):
                    k = half * 4 + kk
                    tr(b.v(c0=kk * 128, c1=kk * 128 + nt), s_.v(c0=k * 128, c1=(k + 1) * 128, p1=nt),
                       IDENT.v(c0=0, c1=nt, p1=nt))
                src_v = View(b.ap2[:, 0:512].rearrange("p (k n) -> p k n", k=4)[:, :, 0:nt], b.v().root, b.v().ivs)
                P.add("act", lambda e, o=X.v(half * 4, half * 4 + 4, c0, c0 + nt).ap, i_=src_v.ap: e.copy(o, i_),
                      reads=[src_v], writes=[X.v(half * 4, half * 4 + 4, c0, c0 + nt)])

        def load_caches(p):
            for l in range(nlayer):
                s_ = STG[l % 2]
                dma("sp", "STG%d" % (l % 2), [(s_.v(c0=0, c1=1024, p1=HR).ap, cache[l, p])], writes=[s_.v()])
                b = bank()
                for k in range(KC):
                    tr(b.v(c0=k * HR, c1=(k + 1) * HR), s_.v(c0=k * 128, c1=(k + 1) * 128, p1=HR), IDENT.v(c0=0, c1=HR, p1=HR))
                src_v = View(b.ap2[:, 0:KC * HR].rearrange("p (k n) -> p k n", k=KC), b.v().root, b.v(c0=0, c1=KC * HR).ivs)
                P.add("act", lambda e, o=HS.v(l * KC, (l + 1) * KC).ap, i_=src_v.ap: e.copy(o, i_),
                      reads=[src_v], writes=[HS.v(l * KC, (l + 1) * KC)])

        def out_tile(p, i, rs):
            (src, nt, c0) = pass_tiles(p)[i]
            si = 0 if c0 < 512 else (1 if c0 < TP else 2)
            t0 = c0 - SUBT[si][0]
            yt = Buf("sb", "T3", A, o_T3, F32, KC, 128)
            for k in range(KC):
                stt(yt.v(k, None, 0, nt), X.v(k, None, c0, c0 + nt), vcol(V_FG + k),
                    STAT.v(si, None, t0, t0 + nt), ALU.mult, ALU.mult)
            s_, key = next_ostg()
            for half in range(2):
                b = bank()
                for kk in range(4):
                    k = half * 4 + kk
                    tr(b.v(c0=kk * 128, c1=(kk + 1) * 128, p1=nt), yt.v(k, None, 0, nt), IDENT.v())
                P.add("act", lambda e, o=s_.v(c0=half * 512, c1=half * 512 + 512, p1=nt).ap, i_=b.v(p1=nt).ap: e.copy(o, i_),
                      reads=[b.v()], writes=[s_.v(c0=half * 512, c1=half * 512 + 512)])
            if c0 < TP:
                dst = yp[p * TP + c0:p * TP + c0 + nt, :]
            else:
                dst = ys[p, 0:nt, :]
            store_ops.append(dma("sp", key, [(dst, s_.v(c0=0, c1=1024, p1=nt).ap)], reads=[s_.v()]))

        NT_ = TP // 128 + 1
        for i in range(NT_):
            load_dma(0, i)
            load_xpose(0, i)
        load_caches(0)
        for p in range(npass):
            seqslots = (0, 1 + p)
            for l in range(nlayer):
                layer(p, l, seqslots, presq=(l > 0))
            rs = rms_all(PA, True)
            nxt = p + 1 < npass
            if nxt:
                load_dma(p + 1, 0)
            for i in range(NT_):
                if nxt and i + 1 < NT_:
                    load_dma(p + 1, i + 1)
                out_tile(p, i, rs)
                if nxt:
                    load_xpose(p + 1, i)
            outs = [(HS, l, sts[l, p]) for l in range(nlayer)]
            if p == npass - 1:
                outs += [(HP, l, stp[l]) for l in range(nlayer)]
            for (HB, l, dst) in outs:
                s_, key = next_ostg()
                for half in range(2):
                    b = bank()
                    for kk in range(4):
                        k = half * 4 + kk
                        tr(b.v(c0=kk * 128, c1=(kk + 1) * 128, p1=HR), HB.v(l * KC + k, None), IDENT.v())
                    P.add("act", lambda e, o=s_.v(c0=half * 512, c1=half * 512 + 512, p1=HR).ap, i_=b.v(p1=HR).ap: e.copy(o, i_),
                          reads=[b.v()], writes=[s_.v(c0=half * 512, c1=half * 512 + 512)])
                store_ops.append(dma("sp", key, [(dst, s_.v(c0=0, c1=1024, p1=HR).ap)], reads=[s_.v()]))
            if nxt:
                load_caches(p + 1)

        P.add("sp", None, extra_deps=store_ops)

        sem_names = ["pe", "act", "dve", "pool"]
        dkeys = ["WR%d" % i for i in range(wring_units)] + ["STG0", "STG1", "OSTG0", "OSTG1"]
        sems = {}
        dsems = {}
        for n_ in sem_names:
            sems[n_] = es.enter_context(nc.semaphore("s_" + n_))
        for k_ in dkeys:
            dsems[k_] = es.enter_context(nc.semaphore("d_" + k_))
        block = es.enter_context(nc.Block())
        mk = P.finalize_and_emit(nc, None, sems, dsems)
        block.sync(mk("sp"))
        block.gpsimd(mk("pool"))
        block.tensor(mk("pe"))
        block.scalar(mk("act"))
        block.vector(mk("dve"))
    return nc, P


def _unit(W, c0, r0=0, nk=8):
    blk = W[r0:r0 + nk * 128, c0:c0 + 128].reshape(nk, 128, 128).transpose(1, 0, 2).reshape(128, nk * 128)
    return blk


def prep_shared(inp):
    wall = np.zeros((NL * U_PER_LAYER, 128, 1024), np.float32)
    wada = np.zeros((NL * 48, 128, 1024), np.float32)
    vecs = np.zeros((V_ROWS, 128), np.float32)
    for l in range(NL):
        base = l * U_PER_LAYER
        w_in = inp["w_in"][l]
        splits = {"bg": 0, "cg": 1, "ha": 2, "ga": 3, "gb": 4, "pin": 5}
        for cb in range(KC):
            for j, nm in enumerate(("bg", "cg", "ha", "ga", "gb", "pin")):
                wall[base + U_CB + cb * 6 + j] = _unit(w_in, splits[nm] * 1024 + cb * 128)
        for ob in range(KC):
            wall[base + U_A2 + ob * 2] = _unit(inp["w_out_a"][l], ob * 128)
            wall[base + U_A2 + ob * 2 + 1] = _unit(w_in, 6 * 1024 + ob * 128)
            wall[base + U_B2 + ob * 2] = _unit(inp["w_out_b"][l], ob * 128)
            wall[base + U_B2 + ob * 2 + 1] = _unit(w_in, 7 * 1024 + ob * 128)
            g = ob // 2
            wall[base + U_C2 + ob * 2, :, 0:256] = _unit(inp["w_pool"][l, g], (ob % 2) * 128, nk=2)
            wall[base + U_C2 + ob * 2 + 1] = _unit(w_in, 8 * 1024 + ob * 128)
            wall[base + U_O + ob] = _unit(inp["w_o"][l], ob * 128)
            wfo = inp["w_ffn_out"][l]
            wall[base + U_FO + ob * 3] = _unit(wfo, ob * 128, 0, 8)
            wall[base + U_FO + ob * 3 + 1] = _unit(wfo, ob * 128, 1024, 8)
            wall[base + U_FO + ob * 3 + 2, :, 0:768] = _unit(wfo, ob * 128, 2048, 6)
        wcb = inp["w_conv_b"][l]
        ar = np.arange(128)
        for cb in range(KC):
            for j in range(31):
                u = base + U_CV + cb * 4 + j // 8
                blk = wall[u].reshape(128, 8, 128)
                blk[ar, j % 8, ar] = wcb[j, cb * 128:(cb + 1) * 128]
        wfi = inp["w_ffn_in"][l]
        for fb in range(NFB):
            wall[base + U_FI + fb * 2] = _unit(wfi, fb * 128)
            wall[base + U_FI + fb * 2 + 1] = _unit(wfi, DFF + fb * 128)
        for j in range(48):
            wada[l * 48 + j] = _unit(inp["w_ada"][l], j * 128)
        vb = l * V_PER_LAYER
        vecs[vb + V_N1:vb + V_N1 + 8] = inp["norm1_g"][l].reshape(8, 128)
        vecs[vb + V_N2:vb + V_N2 + 8] = inp["norm2_g"][l].reshape(8, 128)
        vecs[vb + V_CA:vb + V_CA + 24] = inp["w_conv_a"][l].reshape(24, 128)
        vecs[vb + V_CB:vb + V_CB + 248] = inp["w_conv_b"][l].reshape(248, 128)
        vecs[vb + V_BB:vb + V_BB + 8] = inp["b_conv_b"][l].reshape(8, 128)
        vecs[vb + V_LG:vb + V_LG + 8] = inp["ln_b_g"][l].reshape(8, 128)
        vecs[vb + V_LB:vb + V_LB + 8] = inp["ln_b_b"][l].reshape(8, 128)
        vecs[vb + V_PS:vb + V_PS + 8] = inp["pool_scale"][l].reshape(8, 128)
        vecs[vb + V_BA:vb + V_BA + 48] = inp["b_ada"][l].reshape(48, 128)
    vecs[V_FG:V_FG + 8] = inp["final_g"].reshape(8, 128)
    return wall, wada, vecs


def prep_core(inp, c):
    cache = np.concatenate([inp["cache_conv_a"], inp["cache_conv_b"], inp["cache_pool"]], axis=2)
    return {
        "xp": np.ascontiguousarray(inp["x_prompt"][c]),
        "xs": np.ascontiguousarray(inp["x_sample"][4 * c:4 * c + 4]),
        "cvec": np.ascontiguousarray(np.concatenate([inp["c_prompt"][c:c + 1], inp["c_sample"][4 * c:4 * c + 4]], axis=0)),
        "cache": np.ascontiguousarray(cache[:, 4 * c:4 * c + 4]),
    }


_CACHE = {}


def kernel(**inputs):
    inp = {k: np.asarray(v, dtype=np.float32) for k, v in inputs.items()}
    wall, wada, vecs = prep_shared(inp)
    if "nc" not in _CACHE:
        _CACHE["nc"] = build_program()[0]
    nc = _CACHE["nc"]
    in_maps = []
    for c in range(NCORES):
        m = prep_core(inp, c)
        m.update({"wall": wall, "wada": wada, "vecs": vecs})
        in_maps.append(m)
    res = run_bass_kernel_spmd(nc, in_maps, core_ids=list(range(NCORES)))
    R = res.results
    y_prompt = np.stack([R[c]["yp"] for c in range(NCORES)], axis=0).astype(np.float32)
    y_sample = np.concatenate([R[c]["ys"] for c in range(NCORES)], axis=0).astype(np.float32)
    stp = np.stack([R[c]["stp"] for c in range(NCORES)], axis=1)
    sts = np.concatenate([R[c]["sts"] for c in range(NCORES)], axis=1)
    return (y_prompt, y_sample,
            np.ascontiguousarray(stp[:, :, 0:2]), np.ascontiguousarray(stp[:, :, 2:32]), np.ascontiguousarray(stp[:, :, 32:47]),
            np.ascontiguousarray(sts[:, :, 0:2]), np.ascontiguousarray(sts[:, :, 2:32]), np.ascontiguousarray(sts[:, :, 32:47]))
```
